# Optimizing a Trainium2 kernel written in Bass

```python
import math
import jax, jax.numpy as jnp
from jax import lax
import numpy as np

D_MODEL = 1024
BATCH = 4
SEQ = 8192
DEPTH = 4

GRID_W = 64
CTX_LEN = 256
RMS_EPS = 1e-6
ROPE_THETA = 10000.0
Q_BLOCK = 128

MLA_HEADS = 8
MLA_NOPE = 64
MLA_ROPE = 32
MLA_V = 64
MLA_Q_RANK = 256
MLA_KV_RANK = 128
MLA_WIDTH = MLA_HEADS * MLA_V
MLA_SCALE = (MLA_NOPE + MLA_ROPE) ** -0.5

LRU_WIDTH = 512
LRU_BLOCKS = 8
LRU_BLOCK = LRU_WIDTH // LRU_BLOCKS
LRU_CONV = 4
LRU_C = 8.0

NA_HEADS = 16
NA_HEAD_DIM = 64
NA_WIDTH = NA_HEADS * NA_HEAD_DIM
NA_WIN_R = 8
NA_WIN_C = 16

N_EVEN = (DEPTH + 1) // 2
N_ODD = DEPTH // 2
EVEN_SIZES = (MLA_Q_RANK, MLA_KV_RANK, MLA_ROPE, MLA_WIDTH, LRU_WIDTH, LRU_WIDTH)
EVEN_IN = sum(EVEN_SIZES)
EVEN_SPLITS = tuple(int(v) for v in np.cumsum(EVEN_SIZES)[:-1])
EVEN_MIX = MLA_WIDTH + LRU_WIDTH
ODD_IN = 4 * NA_WIDTH

kernel_name = 'hybrid_mla_rglru_natten_prefix_dit'


def _rmsnorm(x, g):
    xf = x.astype(jnp.float32)
    y = xf * lax.rsqrt(jnp.mean(xf * xf, axis=-1, keepdims=True) + RMS_EPS)
    return (y * g.astype(jnp.float32)).astype(x.dtype)


def _adaln(cond, w, b):
    m = jax.nn.silu(cond) @ w + b
    return jnp.split(m, 3, axis=-1)


def _rope_2d_tables(n_tok):
    n_freq = MLA_ROPE // 4
    inv = ROPE_THETA ** (-jnp.arange(n_freq, dtype=jnp.float32) / n_freq)
    t = jnp.arange(n_tok, dtype=jnp.int32)
    ang_r = (t // GRID_W).astype(jnp.float32)[:, None] * inv
    ang_c = (t % GRID_W).astype(jnp.float32)[:, None] * inv
    return (jnp.cos(ang_r), jnp.sin(ang_r), jnp.cos(ang_c), jnp.sin(ang_c))


def _rotate(x, cos, sin):
    x1, x2 = jnp.split(x, 2, axis=-1)
    return jnp.concatenate([x1 * cos - x2 * sin, x2 * cos + x1 * sin], axis=-1)


def _rope_2d(x, tabs):
    cr, sr, cc, sc = tabs
    xr, xc = jnp.split(x.astype(jnp.float32), 2, axis=-1)
    return jnp.concatenate([_rotate(xr, cr, sr), _rotate(xc, cc, sc)], axis=-1).astype(x.dtype)


def _mla_attend(qn, qr, kn, kr, v):
    s = jnp.einsum('bhqd,bhkd->bhqk', qn, kn) + jnp.einsum('bhqd,bkd->bhqk', qr, kr)
    p = jax.nn.softmax(s.astype(jnp.float32) * MLA_SCALE, axis=-1)
    return jnp.einsum('bhqk,bhkd->bhqd', p.astype(v.dtype), v)


def _short_conv(x, w, b):
    k = w.shape[0]
    y = lax.conv_general_dilated(x, w[:, None, :], window_strides=(1,),
                                 padding=[(k // 2, k - 1 - k // 2)],
                                 dimension_numbers=('NWC', 'WIO', 'NWC'),
                                 feature_group_count=x.shape[-1])
    return y + b


def _rglru_coeffs(x, wa, ba, wx, bx, lam):
    b_, t_, w_ = x.shape
    xb = x.reshape(b_, t_, LRU_BLOCKS, LRU_BLOCK)
    r = jax.nn.sigmoid(jnp.einsum('btnk,nkj->btnj', xb, wa).reshape(b_, t_, w_) + ba)
    i = jax.nn.sigmoid(jnp.einsum('btnk,nkj->btnj', xb, wx).reshape(b_, t_, w_) + bx)
    log_a = -LRU_C * r.astype(jnp.float32) * jax.nn.softplus(-lam.astype(jnp.float32))
    a = jnp.exp(log_a)
    u = jnp.sqrt(-jnp.expm1(2.0 * log_a)) * (i * x).astype(jnp.float32)
    return a, u


def _linear_scan(a, u, h0):
    u = u.at[:, 0].add(a[:, 0] * h0)
    def comb(left, right):
        al, ul = left
        ar, ur = right
        return al * ar, ar * ul + ur
    _, h = lax.associative_scan(comb, (a, u), axis=1)
    return h


def _bidir_rglru(x_c, x_l, wa, ba, wx, bx, lam):
    outs_c, outs_l = [], []
    for d in range(2):
        a_c, u_c = _rglru_coeffs(x_c, wa[d], ba[d], wx[d], bx[d], lam[d])
        a_l, u_l = _rglru_coeffs(x_l, wa[d], ba[d], wx[d], bx[d], lam[d])
        if d == 1:
            a_c, u_c, a_l, u_l = (jnp.flip(t, 1) for t in (a_c, u_c, a_l, u_l))
        h_c = _linear_scan(a_c, u_c, jnp.zeros((x_c.shape[0], x_c.shape[2]), jnp.float32))
        h_l = _linear_scan(a_l, u_l, h_c[:, -1])
        if d == 1:
            h_c, h_l = jnp.flip(h_c, 1), jnp.flip(h_l, 1)
        outs_c.append(h_c)
        outs_l.append(h_l)
    return (outs_c[0] + outs_c[1]).astype(x_c.dtype), (outs_l[0] + outs_l[1]).astype(x_l.dtype)


def _even_mixer(h_c, h_l, w_in, q_norm, w_uq, kv_norm, w_ukv, conv_w, conv_b,
                wa, ba, wx, bx, lam, w_out, tabs, need_ctx):
    b_, c_, _ = h_c.shape
    s_ = h_l.shape[1]
    t_ = c_ + s_
    z = jnp.concatenate([h_c, h_l], axis=1) @ w_in
    cq, ckv, kr, g_mla, x_lru, g_lru = jnp.split(z, EVEN_SPLITS, axis=-1)
    q = (_rmsnorm(cq, q_norm) @ w_uq).reshape(b_, t_, MLA_HEADS, MLA_NOPE + MLA_ROPE).transpose(0, 2, 1, 3)
    kv = (_rmsnorm(ckv, kv_norm) @ w_ukv).reshape(b_, t_, MLA_HEADS, MLA_NOPE + MLA_V).transpose(0, 2, 1, 3)
    q_nope, q_rope = q[..., :MLA_NOPE], q[..., MLA_NOPE:]
    k_nope, v = kv[..., :MLA_NOPE], kv[..., MLA_NOPE:]
    q_rope_l = _rope_2d(q_rope[:, :, c_:], tabs)
    k_rope = jnp.concatenate([kr[:, :c_], _rope_2d(kr[:, c_:], tabs)], axis=1)
    nb = s_ // Q_BLOCK
    def to_blocks(t):
        return jnp.moveaxis(t.reshape(b_, MLA_HEADS, nb, Q_BLOCK, t.shape[-1]), 2, 0)
    o_l = lax.map(lambda qs: _mla_attend(qs[0], qs[1], k_nope, k_rope, v),
                  (to_blocks(q_nope[:, :, c_:]), to_blocks(q_rope_l)))
    o_l = jnp.moveaxis(o_l, 0, 2).reshape(b_, MLA_HEADS, s_, MLA_V).transpose(0, 2, 1, 3).reshape(b_, s_, MLA_WIDTH)
    u_c = _short_conv(x_lru[:, :c_], conv_w, conv_b)
    u_l = _short_conv(x_lru[:, c_:], conv_w, conv_b)
    r_c, r_l = _bidir_rglru(u_c, u_l, wa, ba, wx, bx, lam)
    y_l = jnp.concatenate([o_l * jax.nn.silu(g_mla[:, c_:]), r_l * jax.nn.silu(g_lru[:, c_:])], axis=-1) @ w_out
    y_c = None
    if need_ctx:
        o_c = _mla_attend(q_nope[:, :, :c_], q_rope[:, :, :c_], k_nope[:, :, :c_], k_rope[:, :c_], v[:, :, :c_])
        o_c = o_c.transpose(0, 2, 1, 3).reshape(b_, c_, MLA_WIDTH)
        y_c = jnp.concatenate([o_c * jax.nn.silu(g_mla[:, :c_]), r_c * jax.nn.silu(g_lru[:, :c_])], axis=-1) @ w_out
    return y_c, y_l


def _odd_mixer(h_c, h_l, w_in, rpb, w_out, need_ctx):
    b_, c_, _ = h_c.shape
    s_ = h_l.shape[1]
    rows = s_ // GRID_W
    wr = min(NA_WIN_R, rows)
    z = jnp.concatenate([h_c, h_l], axis=1) @ w_in
    def heads(t):
        return t.reshape(b_, t.shape[1], NA_HEADS, NA_HEAD_DIM).transpose(0, 2, 1, 3)
    q, k, v, g = jnp.split(z, 4, axis=-1)
    q, k, v = heads(q) * (NA_HEAD_DIM ** -0.5), heads(k), heads(v)
    q_c, k_c, v_c = q[:, :, :c_], k[:, :, :c_], v[:, :, :c_]
    q_grid = q[:, :, c_:].reshape(b_, NA_HEADS, rows, GRID_W, NA_HEAD_DIM)
    k_grid = k[:, :, c_:].reshape(b_, NA_HEADS, rows, GRID_W, NA_HEAD_DIM)
    v_grid = v[:, :, c_:].reshape(b_, NA_HEADS, rows, GRID_W, NA_HEAD_DIM)
    cols = jnp.arange(GRID_W)
    col_start = jnp.clip(cols - NA_WIN_C // 2, 0, GRID_W - NA_WIN_C)
    col_idx = col_start[:, None] + jnp.arange(NA_WIN_C)
    dc_idx = col_idx - cols[:, None] + (NA_WIN_C - 1)
    n_loc = wr * NA_WIN_C
    def row_block(args):
        r, q_row = args
        rs = jnp.clip(r - wr // 2, 0, rows - wr)
        k_rows = lax.dynamic_slice_in_dim(k_grid, rs, wr, axis=2)
        v_rows = lax.dynamic_slice_in_dim(v_grid, rs, wr, axis=2)
        k_win = k_rows[:, :, :, col_idx]
        v_win = v_rows[:, :, :, col_idx]
        dr_idx = rs + jnp.arange(wr) - r + (NA_WIN_R - 1)
        bias = rpb[:, dr_idx[None, :, None], dc_idx[:, None, :]]
        s_loc = jnp.einsum('bhqd,bhrqcd->bhqrc', q_row, k_win) + bias
        s_ctx = jnp.einsum('bhqd,bhkd->bhqk', q_row, k_c)
        s = jnp.concatenate([s_loc.reshape(b_, NA_HEADS, GRID_W, n_loc), s_ctx], axis=-1)
        p = jax.nn.softmax(s.astype(jnp.float32), axis=-1).astype(v.dtype)
        p_loc = p[..., :n_loc].reshape(b_, NA_HEADS, GRID_W, wr, NA_WIN_C)
        return (jnp.einsum('bhqrc,bhrqcd->bhqd', p_loc, v_win)
                + jnp.einsum('bhqk,bhkd->bhqd', p[..., n_loc:], v_c))
    o_l = lax.map(row_block, (jnp.arange(rows), jnp.moveaxis(q_grid, 2, 0)))
    o_l = o_l.transpose(1, 0, 3, 2, 4).reshape(b_, s_, NA_WIDTH)
    y_l = (o_l * jax.nn.silu(g[:, c_:])) @ w_out
    y_c = None
    if need_ctx:
        p_c = jax.nn.softmax(jnp.einsum('bhqd,bhkd->bhqk', q_c, k_c).astype(jnp.float32), axis=-1)
        o_c = jnp.einsum('bhqk,bhkd->bhqd', p_c.astype(v.dtype), v_c).transpose(0, 2, 1, 3).reshape(b_, c_, NA_WIDTH)
        y_c = (o_c * jax.nn.silu(g[:, :c_])) @ w_out
    return y_c, y_l


def setup_inputs(seed: int = 0) -> dict:
    key = jax.random.key(seed)
    ks = iter(jax.random.split(key, 32))
    def nrm(shape, s):
        return jax.random.normal(next(ks), shape, jnp.float32) * s
    u = jax.random.uniform(next(ks), (N_EVEN, 2, LRU_WIDTH), jnp.float32, 0.9, 0.999)
    log_a = jnp.log(u) / LRU_C
    lru_lambda = log_a - jnp.log(-jnp.expm1(log_a))
    return {
        'x': nrm((BATCH, SEQ, D_MODEL), 1.0),
        'c': nrm((BATCH, D_MODEL), 1.0),
        'ctx': nrm((BATCH, CTX_LEN, D_MODEL), 1.0),
        'c_ctx': nrm((D_MODEL,), 1.0),
        'ada_w': nrm((DEPTH, D_MODEL, 3 * D_MODEL), 0.5 * D_MODEL ** -0.5),
        'ada_b': nrm((DEPTH, 3 * D_MODEL), 0.01),
        'norm_g': 1.0 + nrm((DEPTH, D_MODEL), 0.01),
        'ev_w_in': nrm((N_EVEN, D_MODEL, EVEN_IN), D_MODEL ** -0.5),
        'mla_q_norm': 1.0 + nrm((N_EVEN, MLA_Q_RANK), 0.01),
        'mla_w_uq': nrm((N_EVEN, MLA_Q_RANK, MLA_HEADS * (MLA_NOPE + MLA_ROPE)), MLA_Q_RANK ** -0.5),
        'mla_kv_norm': 1.0 + nrm((N_EVEN, MLA_KV_RANK), 0.01),
        'mla_w_ukv': nrm((N_EVEN, MLA_KV_RANK, MLA_HEADS * (MLA_NOPE + MLA_V)), MLA_KV_RANK ** -0.5),
        'lru_conv_w': nrm((N_EVEN, LRU_CONV, LRU_WIDTH), LRU_CONV ** -0.5),
        'lru_conv_b': nrm((N_EVEN, LRU_WIDTH), 0.01),
        'lru_wa': nrm((N_EVEN, 2, LRU_BLOCKS, LRU_BLOCK, LRU_BLOCK), LRU_BLOCK ** -0.5),
        'lru_ba': nrm((N_EVEN, 2, LRU_WIDTH), 0.01),
        'lru_wx': nrm((N_EVEN, 2, LRU_BLOCKS, LRU_BLOCK, LRU_BLOCK), LRU_BLOCK ** -0.5),
        'lru_bx': nrm((N_EVEN, 2, LRU_WIDTH), 0.01),
        'lru_lambda': lru_lambda,
        'ev_w_out': nrm((N_EVEN, EVEN_MIX, D_MODEL), EVEN_MIX ** -0.5),
        'od_w_in': nrm((N_ODD, D_MODEL, ODD_IN), D_MODEL ** -0.5),
        'na_rpb': nrm((N_ODD, NA_HEADS, 2 * NA_WIN_R - 1, 2 * NA_WIN_C - 1), 0.1),
        'od_w_out': nrm((N_ODD, NA_WIDTH, D_MODEL), NA_WIDTH ** -0.5),
        'final_norm_g': 1.0 + nrm((D_MODEL,), 0.01),
    }


def reference(x, c, ctx, c_ctx, ada_w, ada_b, norm_g, ev_w_in, mla_q_norm, mla_w_uq,
              mla_kv_norm, mla_w_ukv, lru_conv_w, lru_conv_b, lru_wa, lru_ba, lru_wx, lru_bx,
              lru_lambda, ev_w_out, od_w_in, na_rpb, od_w_out, final_norm_g):
    tabs = _rope_2d_tables(x.shape[1])
    cx = ctx
    for layer in range(DEPTH):
        need_ctx = layer < DEPTH - 1
        shift, scale, gate = _adaln(c, ada_w[layer], ada_b[layer])
        shift_c, scale_c, gate_c = _adaln(c_ctx, ada_w[layer], ada_b[layer])
        h_l = _rmsnorm(x, norm_g[layer]) * (1.0 + scale[:, None]) + shift[:, None]
        h_c = _rmsnorm(cx, norm_g[layer]) * (1.0 + scale_c) + shift_c
        i = layer // 2
        if layer % 2 == 0:
            y_c, y_l = _even_mixer(h_c, h_l, ev_w_in[i], mla_q_norm[i], mla_w_uq[i], mla_kv_norm[i],
                                   mla_w_ukv[i], lru_conv_w[i], lru_conv_b[i], lru_wa[i], lru_ba[i],
                                   lru_wx[i], lru_bx[i], lru_lambda[i], ev_w_out[i], tabs, need_ctx)
        else:
            y_c, y_l = _odd_mixer(h_c, h_l, od_w_in[i], na_rpb[i], od_w_out[i], need_ctx)
        x = x + gate[:, None] * y_l
        if need_ctx:
            cx = cx + gate_c * y_c
    return _rmsnorm(x, final_norm_g)
```

```python
import contextlib
import numpy as np
import concourse.bass as bass
import concourse.mybir as mybir
from concourse.bass_utils import run_bass_kernel_spmd

F32 = mybir.dt.float32
BF16 = mybir.dt.bfloat16
AF = mybir.ActivationFunctionType
ALU = mybir.AluOpType

D = 1024
B = 4
S = 8192
C = 256
T = C + S
DEPTH = 4
GW = 64
ROWS = S // GW
EPS = 1e-6
MLA_H = 4
MLA_SCALE = 96 ** -0.5
EVEN_IN = 1184
NA_H = 8
PAIRS = [[0, 1], [2, 3], [4, 5], [6, 7]]
MASKV = -30000.0
NCORES = 8

BLOCKS = [(0, C)] + [(C + 512 * i, 512) for i in range(S // 512)]


class Sync:
    def __init__(self, nc, es, ndma=10):
        self.nc = nc
        self.eng = {'pe': nc.tensor, 'dve': nc.vector, 'act': nc.scalar, 'pool': nc.gpsimd, 'sp': nc.sync}
        self.sem = {}
        self.cnt = {}
        for e in ('pe', 'dve', 'act', 'pool'):
            self.sem[('e', e)] = es.enter_context(nc.semaphore('s_' + e))
            self.cnt[('e', e)] = 0
        self.dq = {}
        for q in ('sp', 'pool', 'act'):
            keys = []
            for i in range(ndma):
                k = ('d', q, i)
                self.sem[k] = es.enter_context(nc.semaphore('d_%s%d' % (q, i)))
                self.cnt[k] = 0
                keys.append(k)
            self.dq[q] = [keys, 0]
        self.sem[('c',)] = es.enter_context(nc.semaphore('s_cc'))
        self.cnt[('c',)] = 0
        self.waited = {e: {} for e in self.eng}
        self.tok = {}
        self.pending = {e: False for e in self.eng}
        self.ninst = 0

    def _wait(self, e, ev):
        if ev is None:
            return
        k, v = ev
        if k == ('e', e) and e == 'pe':
            return
        if self.waited[e].get(k, 0) >= v:
            return
        self.eng[e].wait_ge(self.sem[k], v)
        self.waited[e][k] = v
        self.ninst += 1

    def _deps(self, e, r, w):
        for t in r:
            st = self.tok.get(t)
            if st is not None:
                self._wait(e, st[0])
        for t in w:
            st = self.tok.get(t)
            if st is not None:
                self._wait(e, st[0])
                for k, v in st[1].items():
                    self._wait(e, (k, v))

    def _record(self, ev, r, w):
        for t in r:
            st = self.tok.setdefault(t, [None, {}])
            if st[1].get(ev[0], 0) < ev[1]:
                st[1][ev[0]] = ev[1]
        for t in w:
            self.tok[t] = [ev, {}]

    def op(self, e, fn, r=(), w=(), sig=True):
        self._deps(e, r, w)
        inst = fn(self.eng[e])
        k = ('e', e)
        ev = (k, self.cnt[k] + 1)
        if sig:
            inst.then_inc(self.sem[k], 1)
            self.cnt[k] += 1
            self.pending[e] = False
        else:
            self.pending[e] = True
        self._record(ev, r, w)
        self.ninst += 1
        return inst

    def dma(self, q, out, in_, r=(), w=()):
        self._deps(q, r, w)
        keys, nxt = self.dq[q]
        k = keys[nxt]
        self.dq[q][1] = (nxt + 1) % len(keys)
        if self.cnt[k] > 0:
            self._wait(q, (k, 16 * self.cnt[k]))
        self.eng[q].dma_start(out=out, in_=in_).then_inc(self.sem[k], 16)
        self.cnt[k] += 1
        ev = (k, 16 * self.cnt[k])
        self._record(ev, r, w)
        self.ninst += 1

    def allreduce(self, out, in_, r=(), w=()):
        self._deps('pool', r, w)
        k = ('c',)
        inst = self.nc.gpsimd.collective_compute("AllReduce", ALU.add, replica_groups=PAIRS,
                                                 ins=[in_.opt()], outs=[out.opt()])
        inst.then_inc(self.sem[k], 1)
        self.cnt[k] += 1
        self._record((k, self.cnt[k]), r, w)
        self.ninst += 1

    def allgather(self, out, in_, r=(), w=()):
        self._deps('pool', r, w)
        k = ('c',)
        inst = self.nc.gpsimd.collective_compute("AllGather", ALU.bypass, replica_groups=PAIRS,
                                                 ins=[in_.opt()], outs=[out.opt()])
        inst.then_inc(self.sem[k], 1)
        self.cnt[k] += 1
        self._record((k, self.cnt[k]), r, w)
        self.ninst += 1

    def barrier(self, engines=('pe', 'dve', 'act', 'pool', 'sp')):
        for e in self.eng:
            assert not self.pending[e], e
        for e in engines:
            for k, c in self.cnt.items():
                if c == 0:
                    continue
                v = c * 16 if k[0] == 'd' else c
                self._wait(e, (k, v))
        self.tok = {}

    def drain_dma(self, e='sp'):
        for k, c in self.cnt.items():
            if k[0] == 'd' and c > 0:
                self._wait(e, (k, 16 * c))


def _valid_irange(qr0, kr):
    ok = []
    for i in range(8):
        r = qr0 + i
        rs = min(max(r - 4, 0), ROWS - 8)
        ok.append(rs <= kr < rs + 8)
    idx = [i for i in range(8) if ok[i]]
    if not idx:
        return None
    assert idx == list(range(idx[0], idx[-1] + 1))
    return idx[0], idx[-1] + 1


def _na_chunks(qb):
    qr0 = 8 * qb
    lo = max(qr0 - 4, 0)
    hi = min(qr0 + 7 + 3, ROWS - 1)
    return list(range(lo // 2, hi // 2 + 1))


def build(layers, final, debug=False):
    nc = bass.Bass("TRN2", target_bir_lowering=False)
    es = contextlib.ExitStack()

    def din(name, shape, dt=F32):
        return nc.dram_tensor(name, list(shape), dt, kind="ExternalInput").ap()

    def dout(name, shape, dt=F32):
        return nc.dram_tensor(name, list(shape), dt, kind="ExternalOutput").ap()

    def dscr(name, shape, dt):
        kind = "ExternalOutput" if debug else "Internal"
        return nc.dram_tensor(name, list(shape), dt, kind=kind).ap()

    xsrc = din("xsrc", [T, D])
    cT_d = din("cT", [128, 8, 2])
    ada_w = din("ada_w", [DEPTH, D, 3 * D])
    ada_b = din("ada_b", [DEPTH, 3 * D])
    norm_g = din("norm_g", [DEPTH, D])
    ev_w_in = din("ev_w_in", [2, D, EVEN_IN])
    ev_w_krp = din("ev_w_krp", [2, D, 32])
    q_norm = din("q_norm", [2, 128, 2])
    kv_norm = din("kv_norm", [2, 128, 1])
    w_uq = din("w_uq", [2, 256, 384])
    w_uqp = din("w_uqp", [2, 256, 384])
    w_ukv = din("w_ukv", [2, 128, 512])
    conv_w = din("conv_w", [2, 128, 2, 4])
    conv_b = din("conv_b", [2, 128, 2])
    lru_wa = din("lru_wa", [2, 2, 4, 64, 64])
    lru_wx = din("lru_wx", [2, 2, 4, 64, 64])
    lru_ba = din("lru_ba", [2, 128, 2, 2])
    lru_bx = din("lru_bx", [2, 128, 2, 2])
    lru_lam = din("lru_lam", [2, 128, 2, 2])
    ev_w_out = din("ev_w_out", [2, D, D])
    od_w_in = din("od_w_in", [2, D, 2048])
    na_ct = din("na_ct", [2, NA_H, 64, 15 * 64])
    od_w_out = din("od_w_out", [2, D, D])
    fin_g = din("fin_g", [128, D])
    ident_d = din("ident", [128, 128])
    sel_d = din("sel", [2, 2, 128])
    cos_d = din("cosT", [32, S])
    sin_d = din("sinT", [32, S])

    if final:
        xdst = dout("out", [S, D])
    else:
        xdst = dout("xdst", [T, D])
    xcur = dscr("xcur", [T, D], F32) if len(layers) > 1 else None
    gT_s = dscr("gT_s", [512, T], BF16)
    oT_s = dscr("oT_s", [512, T], BF16)
    xlT_s = dscr("xlT_s", [256, T], F32)
    NCHK = S // 1024
    mixp = nc.dram_tensor("mixp", [NCHK, 512, 1024], BF16, kind="Internal").ap()
    mixf = nc.dram_tensor("mixf", [NCHK, 1024, 1024], BF16, kind="Internal").ap()
    mixp_c = nc.dram_tensor("mixp_c", [512, C], BF16, kind="Internal").ap()
    mixf_c = nc.dram_tensor("mixf_c", [1024, C], BF16, kind="Internal").ap()
    qT_s = dscr("qT_s", [512, T], BF16)
    kT_s = dscr("kT_s", [512, T], BF16)
    vT_s = dscr("vT_s", [512, T], BF16)

    with es:
        sy = Sync(nc, es)

        uid = [0]

        def sbuf(stack, name, shape, dt):
            uid[0] += 1
            return stack.enter_context(nc.sbuf_tensor("%s_u%d" % (name, uid[0]), list(shape), dt))

        def psum(stack, name, shape, dt=F32):
            uid[0] += 1
            return stack.enter_context(nc.psum_tensor("%s_u%d" % (name, uid[0]), list(shape), dt))

        ident_f = sbuf(es, "ident_f", [128, 128], F32)
        ident_b = sbuf(es, "ident_b", [128, 128], BF16)
        ones_f = sbuf(es, "ones_f", [128, 128], F32)
        sel_sb = sbuf(es, "sel_sb", [2, 2, 128], F32)
        scT = sbuf(es, "scT", [128, 8, 2], F32)
        modt = {}
        for nm in ("A", "Bm", "G"):
            for rr in ("l", "c"):
                modt[(nm, rr)] = sbuf(es, "mod_%s_%s" % (nm, rr), [128, D], F32)
        fbc = sbuf(es, "fbc", [128, D], F32) if final else None

        sy.dma('sp', ident_f[:], ident_d, w=['ident_f'])
        sy.dma('sp', sel_sb[:], sel_d, w=['sel'])
        sy.dma('sp', scT[:], cT_d, w=['scT'])
        sy.op('dve', lambda e: e.tensor_copy(out=ident_b[:], in_=ident_f[:]), r=['ident_f'], w=['ident_b'])
        sy.op('dve', lambda e: e.memset(ones_f[:], 1.0), w=['ones_f'])
        epsb = sbuf(es, "epsb", [128, 1], F32)
        sy.op('dve', lambda e: e.memset(epsb[:], EPS), w=['epsb'])
        sy.op('act', lambda e: e.activation(out=scT[:], in_=scT[:], func=AF.Silu), r=['scT'], w=['scT'])
        if final:
            sy.dma('sp', fbc[:], fin_g, w=['fbc'])

        def x_rows(first, r0, n):
            if first:
                return xsrc[r0:r0 + n, :]
            return xcur[r0:r0 + n, :]

        def phase_mod(L):
            with contextlib.ExitStack() as ps:
                awb = [sbuf(ps, "awb%d" % i, [128, 8, 512], F32) for i in range(2)]
                mrow = sbuf(ps, "mrow", [2, 3 * D], F32)
                grow = sbuf(ps, "grow", [2, D], F32)
                arow = sbuf(ps, "arow", [2, D], F32)
                brow = sbuf(ps, "brow", [1, 3 * D], F32)
                ps_m = [psum(ps, "ps_m%d" % i, [2, 512]) for i in range(2)]
                ps_b = [psum(ps, "ps_b%d" % i, [128, 512]) for i in range(2)]
                sy.dma('sp', brow[:], ada_b[L:L + 1, :], w=['brow'])
                sy.dma('sp', grow[0:1, :], norm_g[L:L + 1, :], w=['grow0'])
                sy.dma('sp', grow[1:2, :], norm_g[L:L + 1, :], w=['grow1'])
                aw = ada_w[L].rearrange("(k p) n -> p k n", p=128)
                for nb in range(6):
                    a = awb[nb % 2]
                    sy.dma('sp', a[:], aw[:, :, nb * 512:(nb + 1) * 512], w=[('awb', nb % 2)])
                    pm = ps_m[nb % 2]
                    for k in range(8):
                        sy.op('pe', lambda e, k=k: e.matmul(pm[:], lhsT=scT[:, k, :], rhs=a[:, k, :],
                                                        start=(k == 0), stop=False),
                              r=[('awb', nb % 2), 'scT'], w=[('ps_m', nb % 2)], sig=False)
                    sy.op('pe', lambda e: e.matmul(pm[:], lhsT=ones_f[0:1, 0:2],
                                                   rhs=brow[0:1, nb * 512:(nb + 1) * 512],
                                                   start=False, stop=True),
                          r=['brow', 'ones_f'], w=[('ps_m', nb % 2)])
                    sy.op('act', lambda e: e.copy(out=mrow[:, nb * 512:(nb + 1) * 512], in_=pm[:]),
                          r=[('ps_m', nb % 2)], w=['mrow'])
                sy.op('dve', lambda e: e.scalar_tensor_tensor(out=arow[:], in0=mrow[:, D:2 * D], scalar=1.0,
                                                              in1=grow[:], op0=ALU.add, op1=ALU.mult),
                      r=['mrow', 'grow0', 'grow1'], w=['arow'])
                i = 0
                for nm, src in (("A", arow[:]), ("Bm", mrow[:, 0:D]), ("G", mrow[:, 2 * D:3 * D])):
                    for ri, rr in enumerate(("l", "c")):
                        for half in range(2):
                            pb = ps_b[i % 2]
                            sy.op('pe', lambda e: e.matmul(pb[:], lhsT=sel_sb[:, ri, :],
                                                           rhs=src[:, half * 512:(half + 1) * 512],
                                                           start=True, stop=True),
                                  r=['sel', 'arow', 'mrow'], w=[('ps_b', i % 2)])
                            sy.op('act', lambda e: e.copy(out=modt[(nm, rr)][:, half * 512:(half + 1) * 512],
                                                          in_=pb[:]),
                                  r=[('ps_b', i % 2)], w=[('mod', nm, rr)])
                            i += 1
                sy.barrier()

        def make_hT(st, first, hT, hbuf, blk, ps_t, tiles):
            r0, ntok = blk
            xt, junk, ssq, rstd, t1, hb = tiles
            for j in range(ntok // 128):
                g = (r0 // 128) + j
                rr = "c" if g < 2 else "l"
                sl = st['xi'] % 2
                st['xi'] += 1
                sy.dma('sp', xt[sl][:], x_rows(first, g * 128, 128), w=[('xt', sl)])
                sy.op('act', lambda e: e.activation(out=junk[:], in_=xt[sl][:], func=AF.Square,
                                                    accum_out=ssq[sl][:]),
                      r=[('xt', sl)], w=['junk', ('ssq', sl)])
                sy.op('act', lambda e: e.activation(out=rstd[sl][:], in_=ssq[sl][:], func=AF.Sqrt,
                                                    scale=1.0 / D, bias=epsb[:, 0:1]),
                      r=[('ssq', sl), 'epsb'], w=[('rstd', sl)])
                sy.op('dve', lambda e: e.reciprocal(out=rstd[sl][:], in_=rstd[sl][:]),
                      r=[('rstd', sl)], w=[('rstd', sl)])
                sy.op('dve', lambda e: e.scalar_tensor_tensor(out=t1[sl][:], in0=xt[sl][:],
                                                              scalar=rstd[sl][:, 0:1],
                                                              in1=modt[("A", rr)][:], op0=ALU.mult,
                                                              op1=ALU.mult),
                      r=[('xt', sl), ('rstd', sl), ('mod', 'A', rr)], w=['t1'])
                sy.op('pool', lambda e: e.tensor_tensor(out=hb[sl][:], in0=t1[sl][:],
                                                        in1=modt[("Bm", rr)][:], op=ALU.add),
                      r=['t1', ('mod', 'Bm', rr)], w=[('hb', sl)])
                pt = ps_t[sl]
                for k in range(8):
                    sy.op('pe', lambda e, k=k: e.transpose(pt[:, k, :], hb[sl][:, k * 128:(k + 1) * 128],
                                                           ident_b[:]),
                          r=[('hb', sl), 'ident_b'], w=[('ps_t', sl)], sig=(k == 7))
                sy.op('act', lambda e: e.copy(out=hT[hbuf][:, :, j * 128:(j + 1) * 128], in_=pt[:]),
                      r=[('ps_t', sl)], w=[('hT', hbuf)])

        def hT_parts(st, first, hT, hbuf, blk, ps_t, tiles):
            r0, ntok = blk
            xt, junk, ssq, rstd, t1, hb = tiles
            parts = []
            for j in range(ntok // 128):
                g = (r0 // 128) + j
                rr = "c" if g < 2 else "l"
                sl = st['xi'] % 2
                st['xi'] += 1

                def part_a(g=g, rr=rr, sl=sl):
                    sy.dma('sp', xt[sl][:], x_rows(first, g * 128, 128), w=[('xt', sl)])
                    sy.op('act', lambda e: e.activation(out=junk[:], in_=xt[sl][:], func=AF.Square,
                                                        accum_out=ssq[sl][:]),
                          r=[('xt', sl)], w=['junk', ('ssq', sl)])
                    sy.op('act', lambda e: e.activation(out=rstd[sl][:], in_=ssq[sl][:], func=AF.Sqrt,
                                                        scale=1.0 / D, bias=epsb[:, 0:1]),
                          r=[('ssq', sl), 'epsb'], w=[('rstd', sl)])
                    sy.op('dve', lambda e: e.reciprocal(out=rstd[sl][:], in_=rstd[sl][:]),
                          r=[('rstd', sl)], w=[('rstd', sl)])
                    sy.op('dve', lambda e: e.scalar_tensor_tensor(out=t1[sl][:], in0=xt[sl][:],
                                                                  scalar=rstd[sl][:, 0:1],
                                                                  in1=modt[("A", rr)][:], op0=ALU.mult,
                                                                  op1=ALU.mult),
                          r=[('xt', sl), ('rstd', sl), ('mod', 'A', rr)], w=['t1'])
                    sy.op('pool', lambda e: e.tensor_tensor(out=hb[sl][:], in0=t1[sl][:],
                                                            in1=modt[("Bm", rr)][:], op=ALU.add),
                          r=['t1', ('mod', 'Bm', rr)], w=[('hb', sl)])

                def part_b(j=j, sl=sl):
                    pt = ps_t[sl]
                    for k in range(8):
                        sy.op('pe', lambda e, k=k: e.transpose(pt[:, k, :], hb[sl][:, k * 128:(k + 1) * 128],
                                                               ident_b[:]),
                              r=[('hb', sl), 'ident_b'], w=[('ps_t', sl)], sig=(k == 7))
                    sy.op('act', lambda e: e.copy(out=hT[hbuf][:, :, j * 128:(j + 1) * 128], in_=pt[:]),
                          r=[('ps_t', sl)], w=[('hT', hbuf)])
                parts.append((part_a, part_b))
            return parts

        def run_blocks(st, first, hT, ps_t, tiles, block_chunks):
            for a_, b_ in hT_parts(st, first, hT, 0, BLOCKS[0], ps_t, tiles):
                a_()
                b_()
            for bi, blk in enumerate(BLOCKS):
                hbuf = bi % 2
                nxt = hT_parts(st, first, hT, (bi + 1) % 2, BLOCKS[bi + 1], ps_t, tiles) if bi + 1 < len(BLOCKS) else []
                slots = []
                for j, (a_, b_) in enumerate(nxt):
                    slots.append([a_] if j == 0 else [nxt[j - 1][1], a_])
                if nxt:
                    slots.append([nxt[-1][1]])
                chunks = block_chunks(bi, blk, hbuf)
                nch, ns = len(chunks), len(slots)
                pos = {}
                for si in range(ns):
                    pos.setdefault((si * nch) // ns, []).append(si)
                for ci, chf in enumerate(chunks):
                    for si in pos.get(ci, []):
                        for f_ in slots[si]:
                            f_()
                    chf()

        def load_cast_weight(ps, dst, src_rows, ncols, tokname, stg, st, nk_=8):
            for k in range(nk_):
                sl = st['wi'] % 2
                st['wi'] += 1
                sy.dma('sp', stg[sl][:, 0:ncols], src_rows(k), w=[('stg', sl)])
                d = dst[:, k, :] if nk_ > 1 else dst[:]
                eng = 'dve' if sl == 0 else 'pool'
                sy.op(eng, lambda e: e.tensor_copy(out=d, in_=stg[sl][:, 0:ncols]),
                      r=[('stg', sl)], w=[tokname])

        def phase_even(L, first, need_ctx):
            i = L // 2
            with contextlib.ExitStack() as lay:
                cqnT = sbuf(lay, "cqnT", [128, 2, T], BF16)
                ckvnT = sbuf(lay, "ckvnT", [128, T], BF16)
                KT = [sbuf(lay, "KT0", [128, T], BF16)]
                with contextlib.ExitStack() as ps:
                    w_in = sbuf(ps, "w_in", [128, 8, EVEN_IN], BF16)
                    wkrp = sbuf(ps, "wkrp", [128, 8, 96], BF16)
                    gq = sbuf(ps, "gq", [128, 2], F32)
                    gkv = sbuf(ps, "gkv", [128, 1], F32)
                    st = {'xi': 0, 'wi': 0}
                    with contextlib.ExitStack() as wst:
                        stg = [sbuf(wst, "stg%d" % b, [128, EVEN_IN], F32) for b in range(2)]
                        load_cast_weight(wst, w_in,
                                         lambda k: ev_w_in[i, k * 128:(k + 1) * 128, :], EVEN_IN, 'w_in', stg, st)
                        sy.op('dve', lambda e: e.memset(wkrp[:], 0.0), w=['wkrp'])
                        for k in range(8):
                            sl = st['wi'] % 2
                            st['wi'] += 1
                            sy.dma('sp', stg[sl][:, 0:32], ev_w_krp[i, k * 128:(k + 1) * 128, :], w=[('stg', sl)])
                            sy.op('dve', lambda e, k=k: e.tensor_copy(out=wkrp[:, k, 64:96], in_=stg[sl][:, 0:32]),
                                  r=[('stg', sl)], w=['wkrp'])
                        sy.dma('sp', gq[:], q_norm[i], w=['gq'])
                        sy.dma('sp', gkv[:], kv_norm[i], w=['gkv'])
                        sy.barrier()
                    xt = [sbuf(ps, "xt%d" % b, [128, D], F32) for b in range(2)]
                    junk = sbuf(ps, "junk", [128, D], BF16)
                    ssq = [sbuf(ps, "ssq%d" % b, [128, 1], F32) for b in range(2)]
                    rstd = [sbuf(ps, "rstd%d" % b, [128, 1], F32) for b in range(2)]
                    t1s = sbuf(ps, "t1_0", [128, D], F32)
                    t1 = [t1s, t1s]
                    hb = [sbuf(ps, "hb%d" % b, [128, D], BF16) for b in range(2)]
                    hT = [sbuf(ps, "hT%d" % b, [128, 8, 512], BF16) for b in range(2)]
                    cqsb = sbuf(ps, "cqsb", [128, 3, 512], F32)
                    sqsb = sbuf(ps, "sqsb", [128, 2, 512], F32)
                    rsb = [sbuf(ps, "rsb%d" % b, [128, 512], F32) for b in range(2)]
                    cs = [sbuf(ps, "cs%d" % b, [128, 512], F32) for b in range(2)]
                    kt1 = sbuf(ps, "kt1", [128, 512], F32)
                    kt2 = sbuf(ps, "kt2", [128, 512], F32)
                    ob = [sbuf(ps, "ob%d" % b, [128, 512], BF16) for b in range(3)]
                    of = [sbuf(ps, "of%d" % b, [128, 512], F32) for b in range(2)]
                    ps_t = [psum(ps, "ps_t%d" % b, [128, 8, 128], BF16) for b in range(2)]
                    ps_a = [psum(ps, "ps_a%d" % b, [128, 512]) for b in range(4)]
                    ps_ss = [psum(ps, "ps_ss%d" % b, [128, 512]) for b in range(2)]
                    cnt = {'a': 0, 'ob': 0, 'of': 0, 'sq': 0}

                    def proj(hbuf, lhs_fn, m, n):
                        b = cnt['a'] % 4
                        cnt['a'] += 1
                        for k in range(8):
                            sy.op('pe', lambda e, k=k: e.matmul(ps_a[b][0:m, 0:n], lhsT=lhs_fn(k),
                                                                rhs=hT[hbuf][:, k, 0:n], start=(k == 0),
                                                                stop=(k == 7)),
                                  r=[('hT', hbuf), 'w_in', 'wkrp'], w=[('ps_a', b)], sig=(k == 7))
                        return b

                    def even_chunks(bi, blk, hbuf):
                        r0, n = blk
                        out = []

                        def norm_group(grp, chunks, gain, nfeat):
                            pss = ps_ss[grp]
                            for ci, cj in enumerate(chunks):
                                b = proj(hbuf, lambda k, cj=cj: w_in[:, k, cj * 128:(cj + 1) * 128], 128, n)
                                sq = cnt['sq'] % 2
                                cnt['sq'] += 1
                                sy.op('act', lambda e: e.copy(out=cqsb[:, cj, 0:n], in_=ps_a[b][:, 0:n]),
                                      r=[('ps_a', b)], w=[('cqsb', cj)])
                                sy.op('act', lambda e: e.activation(out=sqsb[:, sq, 0:n], in_=ps_a[b][:, 0:n],
                                                                    func=AF.Square),
                                      r=[('ps_a', b)], w=[('sqsb', sq)])
                                sy.op('pe', lambda e: e.matmul(pss[:, 0:n], lhsT=ones_f[:], rhs=sqsb[:, sq, 0:n],
                                                               start=(ci == 0), stop=(ci == len(chunks) - 1)),
                                      r=[('sqsb', sq), 'ones_f'], w=[('ps_ss', grp)], sig=True)
                            sy.op('act', lambda e: e.activation(out=rsb[grp][:, 0:n], in_=pss[:, 0:n],
                                                                func=AF.Sqrt, scale=1.0 / nfeat,
                                                                bias=epsb[:, 0:1]),
                                  r=[('ps_ss', grp), 'epsb'], w=[('rsb', grp)])
                            sy.op('dve', lambda e: e.reciprocal(out=rsb[grp][:, 0:n], in_=rsb[grp][:, 0:n]),
                                  r=[('rsb', grp)], w=[('rsb', grp)])
                            for ci, cj in enumerate(chunks):
                                dst = cqnT[:, cj, r0:r0 + n] if grp == 0 else ckvnT[:, r0:r0 + n]
                                sy.op('dve', lambda e: e.scalar_tensor_tensor(out=dst, in0=cqsb[:, cj, 0:n],
                                                                              scalar=gain[:, ci:ci + 1],
                                                                              in1=rsb[grp][:, 0:n],
                                                                              op0=ALU.mult, op1=ALU.mult),
                                      r=[('cqsb', cj), ('rsb', grp), 'gq', 'gkv'],
                                      w=['cqnT' if grp == 0 else 'ckvnT'])
                        out.append(lambda: norm_group(0, (0, 1), gq, 256.0))
                        out.append(lambda: norm_group(1, (2,), gkv, 128.0))

                        def kr_group():
                            b1 = proj(hbuf, lambda k: w_in[:, k, 320:416], 96, n)
                            if r0 < C:
                                sy.op('act', lambda e: e.copy(out=KT[0][64:96, r0:r0 + n], in_=ps_a[b1][64:96, 0:n]),
                                      r=[('ps_a', b1)], w=[('KTr', 0)])
                                return
                            b2 = proj(hbuf, lambda k: wkrp[:, k, :], 96, n)
                            sy.dma('sp', cs[0][64:96, 0:n], cos_d[:, r0 - C:r0 - C + n], w=[('cs', 0)])
                            sy.dma('sp', cs[1][64:96, 0:n], sin_d[:, r0 - C:r0 - C + n], w=[('cs', 1)])
                            sy.op('dve', lambda e: e.tensor_tensor(out=kt1[64:96, 0:n], in0=ps_a[b1][64:96, 0:n],
                                                                   in1=cs[0][64:96, 0:n], op=ALU.mult),
                                  r=[('ps_a', b1), ('cs', 0)], w=['kt1'])
                            sy.op('dve', lambda e: e.tensor_tensor(out=kt2[64:96, 0:n], in0=ps_a[b2][64:96, 0:n],
                                                                   in1=cs[1][64:96, 0:n], op=ALU.mult),
                                  r=[('ps_a', b2), ('cs', 1)], w=['kt2'])
                            sy.op('dve', lambda e: e.tensor_tensor(out=KT[0][64:96, r0:r0 + n],
                                                                   in0=kt1[64:96, 0:n], in1=kt2[64:96, 0:n],
                                                                   op=ALU.add),
                                  r=['kt1', 'kt2'], w=[('KTr', 0)])
                        out.append(kr_group)

                        def gate_chunk(col0, row0):
                            b = proj(hbuf, lambda k: w_in[:, k, col0:col0 + 128], 128, n)
                            o = cnt['ob'] % 3
                            cnt['ob'] += 1
                            sy.op('act', lambda e: e.activation(out=ob[o][:, 0:n], in_=ps_a[b][:, 0:n], func=AF.Silu),
                                  r=[('ps_a', b)], w=[('ob', o)])
                            sy.dma('pool', gT_s[row0:row0 + 128, r0:r0 + n], ob[o][:, 0:n], r=[('ob', o)])

                        def xl_chunk(col0, row0):
                            b = proj(hbuf, lambda k: w_in[:, k, col0:col0 + 128], 128, n)
                            o = cnt['of'] % 2
                            cnt['of'] += 1
                            sy.op('dve', lambda e: e.tensor_copy(out=of[o][:, 0:n], in_=ps_a[b][:, 0:n]),
                                  r=[('ps_a', b)], w=[('of', o)])
                            sy.dma('pool', xlT_s[row0:row0 + 128, r0:r0 + n], of[o][:, 0:n], r=[('of', o)])
                        for cj in range(2):
                            out.append(lambda cj=cj: gate_chunk(416 + cj * 128, cj * 128))
                            out.append(lambda cj=cj: gate_chunk(928 + cj * 128, 256 + cj * 128))
                            out.append(lambda cj=cj: xl_chunk(672 + cj * 128, cj * 128))
                        return out

                    run_blocks(st, first, hT, ps_t, (xt, junk, ssq, rstd, t1, hb), even_chunks)
                    sy.barrier()
                with contextlib.ExitStack() as ps:
                    wuq = sbuf(ps, "wuq", [128, 2, 384], BF16)
                    wuqp = sbuf(ps, "wuqp", [128, 2, 384], BF16)
                    wukv = sbuf(ps, "wukv", [128, 512], BF16)
                    st = {'wi': 0}
                    with contextlib.ExitStack() as wst:
                        stg = [sbuf(wst, "stgb%d" % b, [128, 1024], F32) for b in range(2)]
                        load_cast_weight(wst, wuq, lambda k: w_uq[i, k * 128:(k + 1) * 128, :], 384, 'wuq', stg, st, 2)
                        load_cast_weight(wst, wuqp, lambda k: w_uqp[i, k * 128:(k + 1) * 128, :], 384, 'wuq', stg, st, 2)
                        load_cast_weight(wst, wukv, lambda k: w_ukv[i], 512, 'wukv', stg, st, 1)
                        sy.barrier()
                    KT.append(sbuf(ps, "KT1", [128, T], BF16))
                    sy.op('pool', lambda e: e.tensor_copy(out=KT[1][64:96, :], in_=KT[0][64:96, :]),
                          r=[('KTr', 0)], w=[('KTr', 1)])
                    VP = [sbuf(ps, "VP%d" % b, [128, 66, 65], BF16) for b in range(2)]
                    for b in range(2):
                        sy.op('pool', lambda e: e.memset(VP[b][:, :, 0:1], 1.0), w=[('VP', b)])
                    QT = [sbuf(ps, "QT%d" % b, [128, 512], BF16) for b in range(2)]
                    PB = [sbuf(ps, "PB%d" % b, [128, 512], BF16) for b in range(4)]
                    osb = [sbuf(ps, "osb%d" % b, [128, 512], F32) for b in range(2)]
                    rrow = [sbuf(ps, "rrow%d" % b, [1, 512], F32) for b in range(2)]
                    onb = [sbuf(ps, "onb%d" % b, [128, 512], BF16) for b in range(2)]
                    csq = [[sbuf(ps, "csq%d_%d" % (a, b), [128, 512], F32) for b in range(2)] for a in range(2)]
                    qt1 = sbuf(ps, "qt1", [128, 512], F32)
                    qt2 = sbuf(ps, "qt2", [128, 512], F32)
                    ps_s = [psum(ps, "ps_s%d" % b, [128, 512]) for b in range(4)]
                    ps_o = [psum(ps, "ps_o%d" % b, [128, 512]) for b in range(2)]
                    ps_x = [psum(ps, "ps_x%d" % b, [128, 512]) for b in range(2)]
                    cnt = {'x': 0, 'q': 0, 'it': 0, 'o': 0}

                    def px():
                        b = cnt['x'] % 2
                        cnt['x'] += 1
                        return b

                    def build_kv(h):
                        hb_ = h % 2
                        for (r0, n) in BLOCKS:
                            b = px()
                            sy.op('pe', lambda e: e.matmul(ps_x[b][0:64, 0:n], lhsT=wukv[:, h * 128:h * 128 + 64],
                                                           rhs=ckvnT[:, r0:r0 + n], start=True, stop=True),
                                  r=['wukv', 'ckvnT'], w=[('ps_x', b)])
                            sy.op('dve', lambda e: e.tensor_copy(out=KT[hb_][0:64, r0:r0 + n],
                                                                 in_=ps_x[b][0:64, 0:n]),
                                  r=[('ps_x', b)], w=[('KTn', hb_)])
                        for g0 in range(0, 66, 8):
                            ng = min(8, 66 - g0)
                            b = px()
                            for g in range(ng):
                                t0 = (g0 + g) * 128
                                sy.op('pe', lambda e, g=g, t0=t0: e.matmul(
                                    ps_x[b][:, g * 64:(g + 1) * 64], lhsT=ckvnT[:, t0:t0 + 128],
                                    rhs=wukv[:, h * 128 + 64:h * 128 + 128], start=True, stop=True),
                                      r=['wukv', 'ckvnT'], w=[('ps_x', b)], sig=(g == ng - 1))
                            sy.op('dve', lambda e: e.tensor_copy(
                                out=VP[hb_][:, g0:g0 + ng, 1:65],
                                in_=ps_x[b][:, 0:ng * 64].rearrange("p (g d) -> p g d", d=64)),
                                  r=[('ps_x', b)], w=[('VP', hb_)])

                    def build_q(h, qblk):
                        r0, n = qblk
                        qb_ = cnt['q'] % 2
                        cnt['q'] += 1
                        b1 = px()
                        for j in range(2):
                            sy.op('pe', lambda e, j=j: e.matmul(ps_x[b1][0:96, 0:n],
                                                                lhsT=wuq[:, j, h * 96:(h + 1) * 96],
                                                                rhs=cqnT[:, j, r0:r0 + n], start=(j == 0),
                                                                stop=(j == 1)),
                                  r=['wuq', 'cqnT'], w=[('ps_x', b1)], sig=(j == 1))
                        if r0 < C:
                            sy.op('dve', lambda e: e.tensor_copy(out=QT[qb_][0:96, 0:n], in_=ps_x[b1][0:96, 0:n]),
                                  r=[('ps_x', b1)], w=[('QT', qb_)])
                            return qb_
                        b2 = px()
                        for j in range(2):
                            sy.op('pe', lambda e, j=j: e.matmul(ps_x[b2][0:96, 0:n],
                                                                lhsT=wuqp[:, j, h * 96:(h + 1) * 96],
                                                                rhs=cqnT[:, j, r0:r0 + n], start=(j == 0),
                                                                stop=(j == 1)),
                                  r=['wuq', 'cqnT'], w=[('ps_x', b2)], sig=(j == 1))
                        sy.dma('sp', csq[0][qb_][64:96, 0:n], cos_d[:, r0 - C:r0 - C + n], w=[('csq', 0, qb_)])
                        sy.dma('sp', csq[1][qb_][64:96, 0:n], sin_d[:, r0 - C:r0 - C + n], w=[('csq', 1, qb_)])
                        sy.op('dve', lambda e: e.tensor_copy(out=QT[qb_][0:64, 0:n], in_=ps_x[b1][0:64, 0:n]),
                              r=[('ps_x', b1)], w=[('QT', qb_)])
                        sy.op('dve', lambda e: e.tensor_tensor(out=qt1[64:96, 0:n], in0=ps_x[b1][64:96, 0:n],
                                                               in1=csq[0][qb_][64:96, 0:n], op=ALU.mult),
                              r=[('ps_x', b1), ('csq', 0, qb_)], w=['qt1'])
                        sy.op('dve', lambda e: e.tensor_tensor(out=qt2[64:96, 0:n], in0=ps_x[b2][64:96, 0:n],
                                                               in1=csq[1][qb_][64:96, 0:n], op=ALU.mult),
                              r=[('ps_x', b2), ('csq', 1, qb_)], w=['qt2'])
                        sy.op('dve', lambda e: e.tensor_tensor(out=QT[qb_][64:96, 0:n], in0=qt1[64:96, 0:n],
                                                               in1=qt2[64:96, 0:n], op=ALU.add),
                              r=['qt1', 'qt2'], w=[('QT', qb_)])
                        return qb_

                    def epilogue(ob_, h, qblk):
                        r0, n = qblk
                        sy.op('act', lambda e: e.copy(out=osb[ob_][0:65, 0:n], in_=ps_o[ob_][0:65, 0:n]),
                              r=[('ps_o', ob_)], w=[('osb', ob_)])
                        sy.op('dve', lambda e: e.reciprocal(out=rrow[ob_][0:1, 0:n], in_=osb[ob_][0:1, 0:n]),
                              r=[('osb', ob_)], w=[('rrow', ob_)])

                        def stage2():
                            b = px()
                            sy.op('pe', lambda e: e.matmul(ps_x[b][0:65, 0:n], lhsT=ones_f[0:1, 0:65],
                                                           rhs=rrow[ob_][0:1, 0:n], start=True, stop=True),
                                  r=[('rrow', ob_), 'ones_f'], w=[('ps_x', b)])
                            sy.op('dve', lambda e: e.tensor_tensor(out=onb[ob_][0:65, 0:n],
                                                                   in0=osb[ob_][0:65, 0:n],
                                                                   in1=ps_x[b][0:65, 0:n], op=ALU.mult),
                                  r=[('osb', ob_), ('ps_x', b)], w=[('onb', ob_)])
                            sy.dma('pool', oT_s[h * 64:(h + 1) * 64, r0:r0 + n], onb[ob_][1:65, 0:n],
                                   r=[('onb', ob_)])
                        pend.append((tnow[0] + 3, stage2))

                    qblocks = ([BLOCKS[0]] if need_ctx else []) + BLOCKS[1:]
                    items = []
                    for h in range(MLA_H):
                        for qblk in qblocks:
                            nk = 2 if qblk[0] < C else 66
                            for c in range(nk):
                                items.append((h, qblk, c, nk))
                    LA = 3
                    state = {}
                    build_kv(0)
                    pend = []
                    tnow = [0]
                    for t in range(len(items) + LA + 4):
                        tnow[0] = t
                        while pend and pend[0][0] <= t:
                            pend.pop(0)[1]()
                        if t < len(items):
                            h, qblk, c, nk = items[t]
                            n = qblk[1]
                            hb_ = h % 2
                            if c == 0:
                                if t == 0:
                                    state[(h, qblk)] = build_q(h, qblk)
                                if t + nk < len(items):
                                    h2, qblk2 = items[t + nk][0], items[t + nk][1]
                                    state[(h2, qblk2)] = build_q(h2, qblk2)
                                if qblk == qblocks[1 if len(qblocks) > 1 else 0] and h + 1 < MLA_H:
                                    build_kv(h + 1)
                            qb_ = state[(h, qblk)]
                            sb_ = t % 4
                            sy.op('pe', lambda e: e.matmul(ps_s[sb_][:, 0:n], lhsT=KT[hb_][0:96, c * 128:(c + 1) * 128],
                                                           rhs=QT[qb_][0:96, 0:n], start=True, stop=True),
                                  r=[('KTn', hb_), ('KTr', hb_), ('QT', qb_)], w=[('ps_s', sb_)])
                            sy.op('act', lambda e: e.activation(out=PB[sb_][:, 0:n], in_=ps_s[sb_][:, 0:n],
                                                                func=AF.Exp, scale=MLA_SCALE),
                                  r=[('ps_s', sb_)], w=[('PB', sb_)])
                        u = t - LA
                        if 0 <= u < len(items):
                            h, qblk, c, nk = items[u]
                            n = qblk[1]
                            hb_ = h % 2
                            if c == 0:
                                state[('o', h, qblk)] = cnt['o'] % 2
                                cnt['o'] += 1
                            ob_ = state[('o', h, qblk)]
                            sb_ = u % 4
                            sy.op('pe', lambda e: e.matmul(ps_o[ob_][0:65, 0:n], lhsT=VP[hb_][:, c, :],
                                                           rhs=PB[sb_][:, 0:n], start=(c == 0), stop=(c == nk - 1)),
                                  r=[('VP', hb_), ('PB', sb_)], w=[('ps_o', ob_)], sig=True)
                            if c == nk - 1:
                                epilogue(ob_, h, qblk)
                    assert not pend
                    sy.barrier()
            with contextlib.ExitStack() as ps:
                XL = sbuf(ps, "XL", [128, T], F32)
                U = sbuf(ps, "U", [128, T], F32)
                HS = sbuf(ps, "HS", [128, T], F32)
                OB = sbuf(ps, "OB", [128, T], BF16)
                PN = 1024
                Rt = [sbuf(ps, "Rt%d" % b, [128, PN], F32) for b in range(2)]
                It = [sbuf(ps, "It%d" % b, [128, PN], F32) for b in range(2)]
                At = [sbuf(ps, "At%d" % b, [128, PN], F32) for b in range(2)]
                Mt = [sbuf(ps, "Mt%d" % b, [128, PN], F32) for b in range(2)]
                Hp = [sbuf(ps, "Hp%d" % b, [128, PN], F32) for b in range(2)]
                carry = sbuf(ps, "carry", [128, 2], F32)
                wbd = [[sbuf(ps, "wbd%d_%d" % (d, g), [128, 128], F32) for g in range(2)] for d in range(2)]
                cw = sbuf(ps, "cw", [128, 2, 4], F32)
                cb = sbuf(ps, "cb", [128, 2], F32)
                bat = sbuf(ps, "bat", [128, 2, 2], F32)
                bxt = sbuf(ps, "bxt", [128, 2, 2], F32)
                lamt = sbuf(ps, "lamt", [128, 2, 2], F32)
                clt = sbuf(ps, "clt", [128, 2, 2], F32)
                cl2t = sbuf(ps, "cl2t", [128, 2, 2], F32)
                ps_r = [psum(ps, "ps_r%d" % b, [128, PN]) for b in range(2)]
                ps_i = [psum(ps, "ps_i%d" % b, [128, PN]) for b in range(2)]
                sy.dma('sp', cw[:], conv_w[i], w=['cw'])
                sy.dma('sp', cb[:], conv_b[i], w=['cb'])
                sy.dma('sp', bat[:], lru_ba[i], w=['bat'])
                sy.dma('sp', bxt[:], lru_bx[i], w=['bxt'])
                sy.dma('sp', lamt[:], lru_lam[i], w=['lamt'])
                sy.op('act', lambda e: e.activation(out=clt[:], in_=lamt[:], func=AF.Exp, scale=-1.0),
                      r=['lamt'], w=['clt'])
                sy.op('act', lambda e: e.activation(out=clt[:], in_=clt[:], func=AF.Ln, bias=1.0),
                      r=['clt'], w=['clt'])
                sy.op('dve', lambda e: e.tensor_scalar(out=cl2t[:], in0=clt[:], scalar1=-16.0, scalar2=None,
                                                       op0=ALU.mult), r=['clt'], w=['cl2t'])
                sy.op('dve', lambda e: e.tensor_scalar(out=clt[:], in0=clt[:], scalar1=-8.0, scalar2=None,
                                                       op0=ALU.mult), r=['clt', 'cl2t'], w=['clt'])
                pieces = [(0, C)] + [(C + PN * k, PN) for k in range(S // PN)]
                def rv(t_, n_):
                    full = t_[:]
                    return bass.AP(t_, n_ - 1, [[full.ap[0][0], 128], [-1, n_]])

                def prologue(cc):
                    sy.dma('sp', XL[:], xlT_s[cc * 128:(cc + 1) * 128, :], w=['XL'])
                    for d in range(2):
                        for g, wsrc in enumerate((lru_wa, lru_wx)):
                            sy.op('dve', lambda e: e.memset(wbd[d][g][:], 0.0), w=[('wbd', d, g)])
                            for hf in range(2):
                                sy.dma('sp', wbd[d][g][hf * 64:(hf + 1) * 64, hf * 64:(hf + 1) * 64],
                                       wsrc[i, d, cc * 2 + hf], w=[('wbd', d, g)])
                    for (s0, e0) in ((0, C), (C, T)):
                        sy.op('dve', lambda e: e.tensor_scalar(out=U[:, s0:e0], in0=XL[:, s0:e0],
                                                               scalar1=cw[:, cc, 2:3], scalar2=cb[:, cc:cc + 1],
                                                               op0=ALU.mult, op1=ALU.add),
                              r=['XL', 'cw', 'cb'], w=['U'])
                        for j, off in ((0, -2), (1, -1), (3, 1)):
                            if off < 0:
                                o_sl = (s0 - off, e0)
                                i_sl = (s0, e0 + off)
                            else:
                                o_sl = (s0, e0 - off)
                                i_sl = (s0 + off, e0)
                            sy.op('dve', lambda e: e.scalar_tensor_tensor(
                                out=U[:, o_sl[0]:o_sl[1]], in0=XL[:, i_sl[0]:i_sl[1]], scalar=cw[:, cc, j:j + 1],
                                in1=U[:, o_sl[0]:o_sl[1]], op0=ALU.mult, op1=ALU.add),
                                  r=['XL', 'cw', 'U'], w=['U'])

                def stage1(cc, d, pi, p0, pn, sl):
                    for q0 in range(0, pn, 512):
                        qn = min(512, pn - q0)
                        for g, pst in ((0, ps_r), (1, ps_i)):
                            sy.op('pe', lambda e: e.matmul(pst[sl][:, q0:q0 + qn], lhsT=wbd[d][g][:],
                                                           rhs=U[:, p0 + q0:p0 + q0 + qn], start=True, stop=True),
                                  r=['U', ('wbd', d, g)], w=[('psg', g, sl)])
                    sy.op('act', lambda e: e.activation(out=Rt[sl][:, 0:pn], in_=ps_r[sl][:, 0:pn],
                                                        func=AF.Sigmoid, bias=bat[:, d, cc:cc + 1]),
                          r=[('psg', 0, sl), 'bat'], w=[('Rt', sl)])
                    sy.op('act', lambda e: e.activation(out=It[sl][:, 0:pn], in_=ps_i[sl][:, 0:pn],
                                                        func=AF.Sigmoid, bias=bxt[:, d, cc:cc + 1]),
                          r=[('psg', 1, sl), 'bxt'], w=[('It', sl)])
                    sy.op('act', lambda e: e.activation(out=At[sl][:, 0:pn], in_=Rt[sl][:, 0:pn],
                                                        func=AF.Exp, scale=clt[:, d, cc:cc + 1]),
                          r=[('Rt', sl), 'clt'], w=[('At', sl)])
                    sy.op('act', lambda e: e.activation(out=Mt[sl][:, 0:pn], in_=Rt[sl][:, 0:pn],
                                                        func=AF.Exp, scale=cl2t[:, d, cc:cc + 1]),
                          r=[('Rt', sl), 'cl2t'], w=[('Mt', sl)])
                    sy.op('dve', lambda e: e.tensor_scalar(out=Mt[sl][:, 0:pn], in0=Mt[sl][:, 0:pn],
                                                           scalar1=-1.0, scalar2=1.0, op0=ALU.mult, op1=ALU.add),
                          r=[('Mt', sl)], w=[('Mt', sl)])
                    sy.op('dve', lambda e: e.tensor_tensor(out=It[sl][:, 0:pn], in0=It[sl][:, 0:pn],
                                                           in1=U[:, p0:p0 + pn], op=ALU.mult),
                          r=[('It', sl), 'U'], w=[('It', sl)])

                def stage2(cc, d, pi, p0, pn, sl):
                    sy.op('act', lambda e: e.activation(out=Mt[sl][:, 0:pn], in_=Mt[sl][:, 0:pn], func=AF.Sqrt),
                          r=[('Mt', sl)], w=[('Mt', sl)])
                    sy.op('dve', lambda e: e.tensor_tensor(out=It[sl][:, 0:pn], in0=It[sl][:, 0:pn],
                                                           in1=Mt[sl][:, 0:pn], op=ALU.mult),
                          r=[('It', sl), ('Mt', sl)], w=[('It', sl)])
                    init = 0.0 if pi == 0 else carry[:, d:d + 1]
                    if d == 0:
                        sy.op('dve', lambda e: e.tensor_tensor_scan(
                            out=HS[:, p0:p0 + pn], data0=At[sl][:, 0:pn], data1=It[sl][:, 0:pn],
                            initial=init, op0=ALU.mult, op1=ALU.add),
                              r=[('At', sl), ('It', sl), 'carry'], w=['HS'])
                        sy.op('dve', lambda e: e.tensor_copy(out=carry[:, 0:1], in_=HS[:, p0 + pn - 1:p0 + pn]),
                              r=['HS'], w=['carry'])
                    else:
                        sy.op('dve', lambda e: e.tensor_tensor_scan(
                            out=rv(Hp[sl], pn), data0=rv(At[sl], pn), data1=rv(It[sl], pn),
                            initial=init, op0=ALU.mult, op1=ALU.add),
                              r=[('At', sl), ('It', sl), 'carry'], w=[('Hp', sl)])
                        sy.op('dve', lambda e: e.tensor_copy(out=carry[:, 1:2], in_=Hp[sl][:, 0:1]),
                              r=[('Hp', sl)], w=['carry'])
                        sy.op('pool', lambda e: e.tensor_tensor(out=OB[:, p0:p0 + pn], in0=HS[:, p0:p0 + pn],
                                                                in1=Hp[sl][:, 0:pn], op=ALU.add),
                              r=['HS', ('Hp', sl)], w=['OB'])
                    if d == 1 and pi == len(pieces) - 1:
                        sy.dma('pool', oT_s[256 + cc * 128:256 + (cc + 1) * 128, :], OB[:], r=['OB'])

                work = []
                for cc in range(2):
                    for d in range(2):
                        order = pieces if d == 0 else [pieces[0]] + pieces[:0:-1]
                        for pi, (p0, pn) in enumerate(order):
                            work.append((cc, d, pi, p0, pn, len(work) % 2))
                for wi in range(len(work) + 1):
                    if wi < len(work):
                        if work[wi][1] == 0 and work[wi][2] == 0:
                            prologue(work[wi][0])
                        stage1(*work[wi])
                    if wi >= 1:
                        stage2(*work[wi - 1])
                sy.barrier()

        def phase_odd(L, first, need_ctx):
            i = L // 2
            with contextlib.ExitStack() as ps:
                w_in = sbuf(ps, "w_ino", [128, 8, 2048], BF16)
                st = {'xi': 0, 'wi': 0}
                with contextlib.ExitStack() as wst:
                    stg = [sbuf(wst, "stgo%d" % b, [128, 2048], F32) for b in range(2)]
                    for hf in range(1):
                        for k in range(8):
                            sl = st['wi'] % 2
                            st['wi'] += 1
                            sy.dma('sp', stg[sl][:], od_w_in[i, k * 128:(k + 1) * 128, hf * 2048:(hf + 1) * 2048],
                                   w=[('stg', sl)])
                            sy.op('dve' if sl == 0 else 'pool',
                                  lambda e, k=k: e.tensor_copy(out=w_in[:, k, hf * 2048:(hf + 1) * 2048],
                                                               in_=stg[sl][:]),
                                  r=[('stg', sl)], w=['w_in'])
                    sy.barrier()
                xt = [sbuf(ps, "xto%d" % b, [128, D], F32) for b in range(2)]
                junk = sbuf(ps, "junko", [128, D], BF16)
                ssq = [sbuf(ps, "ssqo%d" % b, [128, 1], F32) for b in range(2)]
                rstd = [sbuf(ps, "rstdo%d" % b, [128, 1], F32) for b in range(2)]
                t1s = sbuf(ps, "t1o_0", [128, D], F32)
                t1 = [t1s, t1s]
                hb = [sbuf(ps, "hbo%d" % b, [128, D], BF16) for b in range(2)]
                hT = [sbuf(ps, "hTo%d" % b, [128, 8, 512], BF16) for b in range(2)]
                ob = [sbuf(ps, "obo%d" % b, [128, 512], BF16) for b in range(4)]
                ps_t = [psum(ps, "ps_to%d" % b, [128, 8, 128], BF16) for b in range(2)]
                ps_a = [psum(ps, "ps_ao%d" % b, [128, 512]) for b in range(4)]
                cnt = {'a': 0, 'ob': 0, 'vb': 0}
                def odd_chunks(bi, blk, hbuf):
                    r0, n = blk
                    out = []

                    def one(which, col0, dst, cj):
                        b = cnt['a'] % 4
                        cnt['a'] += 1
                        for k in range(8):
                            sy.op('pe', lambda e, k=k: e.matmul(
                                ps_a[b][:, 0:n], lhsT=w_in[:, k, col0 + cj * 128:col0 + (cj + 1) * 128],
                                rhs=hT[hbuf][:, k, 0:n], start=(k == 0), stop=(k == 7)),
                                  r=[('hT', hbuf), 'w_in'], w=[('ps_a', b)], sig=(k == 7))
                        o = cnt['ob'] % 4
                        cnt['ob'] += 1
                        if which == "q":
                            sy.op('dve', lambda e: e.tensor_scalar(out=ob[o][:, 0:n], in0=ps_a[b][:, 0:n],
                                                                   scalar1=0.125, scalar2=None, op0=ALU.mult),
                                  r=[('ps_a', b)], w=[('ob', o)])
                        elif which == "k":
                            sy.op('dve', lambda e: e.tensor_copy(out=ob[o][:, 0:n], in_=ps_a[b][:, 0:n]),
                                  r=[('ps_a', b)], w=[('ob', o)])
                        elif which == "v":
                            sy.op('act', lambda e: e.copy(out=ob[o][:, 0:n], in_=ps_a[b][:, 0:n]),
                                  r=[('ps_a', b)], w=[('ob', o)])
                        else:
                            sy.op('act', lambda e: e.activation(out=ob[o][:, 0:n], in_=ps_a[b][:, 0:n],
                                                                func=AF.Silu),
                                  r=[('ps_a', b)], w=[('ob', o)])
                        sy.dma('pool', dst[cj * 128:(cj + 1) * 128, r0:r0 + n], ob[o][:, 0:n], r=[('ob', o)])
                    for cj in range(4):
                        for which, col0, dst in (("q", 0, qT_s), ("k", 512, kT_s), ("v", 1024, vT_s),
                                                 ("g", 1536, gT_s)):
                            out.append(lambda which=which, col0=col0, dst=dst, cj=cj: one(which, col0, dst, cj))
                    return out

                run_blocks(st, first, hT, ps_t, (xt, junk, ssq, rstd, t1, hb), odd_chunks)
                sy.barrier()
            with contextlib.ExitStack() as ps:
                KTp = [sbuf(ps, "KTp%d" % b, [128, T], BF16) for b in range(2)]
                VTp = sbuf(ps, "VTp", [128, T], BF16)
                VP = [sbuf(ps, "VPn%d" % b, [128, 66, 65], BF16) for b in range(4)]
                CT = [sbuf(ps, "CT%d" % b, [128, 15 * 64], F32) for b in range(2)]
                TAB = [sbuf(ps, "TAB%d" % b, [128, 20, 512], BF16) for b in range(2)]
                QT = [[sbuf(ps, "QTn%d_%d" % (s_, b), [128, 512], BF16) for b in range(2)] for s_ in range(2)]
                PB = [sbuf(ps, "PBn%d" % b, [128, 512], BF16) for b in range(4)]
                osb = [sbuf(ps, "osbn%d" % b, [128, 512], F32) for b in range(2)]
                rrow = [sbuf(ps, "rrown%d" % b, [1, 512], F32) for b in range(2)]
                onb = [sbuf(ps, "onbn%d" % b, [128, 512], BF16) for b in range(2)]
                ps_s = [psum(ps, "ps_sn%d" % b, [128, 512]) for b in range(4)]
                ps_o = [psum(ps, "ps_on%d" % b, [128, 512]) for b in range(2)]
                ps_x = [psum(ps, "ps_xn0", [128, 512])]
                ps_tr = psum(ps, "ps_trn", [128, 8, 128], BF16)
                cnt = {'x': 0, 'q': 0, 'o': 0}
                for b in range(4):
                    sy.op('pool', lambda e: e.memset(VP[b][:, :, 0:1], 1.0), w=[('VP', b)])
                for b in range(2):
                    sy.op('pool', lambda e: e.memset(TAB[b][:], MASKV), w=[('TAB', b)])
                    for s_ in range(2):
                        sy.op('pool', lambda e: e.memset(QT[s_][b][:], 0.0), w=[('QT', s_, b)])
                slot_of = {}
                slots = []
                for typ, qb in ((0, 0), (1, 5), (2, 15)):
                    for ci, ch in enumerate(_na_chunks(qb)):
                        slot_of[(typ, ci)] = len(slots)
                        slots.append((qb, ch))
                assert len(slots) == 20

                def prep_dma(h):
                    hb_ = h % 2
                    if h % 2 == 0:
                        hp = h // 2
                        sy.dma('sp', KTp[hp % 2][:], kT_s[hp * 128:(hp + 1) * 128, :], w=[('KTp', hp % 2)])
                        sy.dma('sp', VTp[:], vT_s[hp * 128:(hp + 1) * 128, :], w=['VTp'])
                    for hf in range(2):
                        sy.dma('sp', CT[hb_][hf * 64:(hf + 1) * 64, :], na_ct[i, h], w=[('CT', hb_)])

                def prep_head(h):
                    hb_ = h % 2
                    if h % 2 == 0:
                        hp = h // 2
                        pb_ = (h // 2) % 2
                        for g0 in range(0, 66, 8):
                            ng = min(8, 66 - g0)
                            for g in range(ng):
                                t0 = (g0 + g) * 128
                                sy.op('pe', lambda e, g=g, t0=t0: e.transpose(ps_tr[:, g, :], VTp[:, t0:t0 + 128],
                                                                              ident_b[:]),
                                      r=['VTp', 'ident_b'], w=['ps_tr'], sig=(g == ng - 1))
                            for s2 in range(2):
                                sy.op('dve', lambda e: e.tensor_copy(out=VP[pb_ * 2 + s2][:, g0:g0 + ng, 1:65],
                                                                     in_=ps_tr[:, 0:ng, s2 * 64:(s2 + 1) * 64]),
                                      r=['ps_tr'], w=[('VP', pb_ * 2 + s2)])
                    for si, (qb, ch) in enumerate(slots):
                        for a in range(2):
                            kr = 2 * ch + a
                            vr = _valid_irange(8 * qb, kr)
                            if vr is None:
                                continue
                            i0, i1 = vr
                            e0 = 7 - kr + 8 * qb + i0
                            assert 0 <= e0 and e0 + (i1 - i0) <= 15
                            sy.op('dve', lambda e: e.tensor_copy(
                                out=TAB[hb_][a * 64:(a + 1) * 64, si, i0 * 64:i1 * 64],
                                in_=CT[hb_][a * 64:(a + 1) * 64, e0 * 64:(e0 + i1 - i0) * 64]),
                                  r=[('CT', hb_)], w=[('TAB', hb_)])

                def px():
                    return 0

                def epilogue(ob_, h, qblk):
                    r0, n = qblk
                    sy.op('act', lambda e: e.copy(out=osb[ob_][0:65, 0:n], in_=ps_o[ob_][0:65, 0:n]),
                          r=[('ps_o', ob_)], w=[('osb', ob_)])
                    sy.op('dve', lambda e: e.reciprocal(out=rrow[ob_][0:1, 0:n], in_=osb[ob_][0:1, 0:n]),
                          r=[('osb', ob_)], w=[('rrow', ob_)])

                    def stage2():
                        b = px()
                        sy.op('pe', lambda e: e.matmul(ps_x[b][0:65, 0:n], lhsT=ones_f[0:1, 0:65],
                                                       rhs=rrow[ob_][0:1, 0:n], start=True, stop=True),
                              r=[('rrow', ob_), 'ones_f'], w=[('ps_x', b)])
                        sy.op('dve', lambda e: e.tensor_tensor(out=onb[ob_][0:65, 0:n], in0=osb[ob_][0:65, 0:n],
                                                               in1=ps_x[b][0:65, 0:n], op=ALU.mult),
                              r=[('osb', ob_), ('ps_x', b)], w=[('onb', ob_)])
                        sy.dma('pool', oT_s[h * 64:(h + 1) * 64, r0:r0 + n], onb[ob_][1:65, 0:n],
                               r=[('onb', ob_)])
                    pend.append((tnow[0] + 3, stage2))

                qblocks = ([BLOCKS[0]] if need_ctx else []) + BLOCKS[1:]
                items = []
                for h in range(NA_H):
                    for qblk in qblocks:
                        if qblk[0] < C:
                            chs = [(0, None), (1, None)]
                        else:
                            qb = (qblk[0] - C) // 512
                            typ = 0 if qb == 0 else (2 if qb == 15 else 1)
                            chs = [(0, None), (1, None)] + [(2 + ch, slot_of[(typ, ci)])
                                                            for ci, ch in enumerate(_na_chunks(qb))]
                        for ci, (kt, slot) in enumerate(chs):
                            items.append((h, qblk, ci, len(chs), kt, slot))
                LA = 3
                state = {}
                prep_dma(0)
                prep_head(0)
                pend = []
                tnow = [0]
                for t in range(len(items) + LA + 4):
                    tnow[0] = t
                    while pend and pend[0][0] <= t:
                        pend.pop(0)[1]()
                    if t < len(items):
                        h, qblk, ci, nk, kt, slot = items[t]
                        r0, n = qblk
                        hb_ = h % 2
                        s_ = h % 2
                        kp = (h // 2) % 2
                        if ci == 0:
                            def load_q(h2, qblk2):
                                q2 = cnt['q'] % 2
                                cnt['q'] += 1
                                state[(h2, qblk2)] = q2
                                s2_ = h2 % 2
                                sy.dma('sp', QT[s2_][q2][s2_ * 64:(s2_ + 1) * 64, 0:qblk2[1]],
                                       qT_s[h2 * 64:(h2 + 1) * 64, qblk2[0]:qblk2[0] + qblk2[1]],
                                       w=[('QT', s2_, q2)])
                            if t == 0:
                                load_q(h, qblk)
                            if t + nk < len(items):
                                load_q(items[t + nk][0], items[t + nk][1])
                            if qblk == qblocks[1 if len(qblocks) > 1 else 0] and h + 1 < NA_H:
                                prep_dma(h + 1)
                            if qblk == qblocks[min(9, len(qblocks) - 1)] and h + 1 < NA_H:
                                prep_head(h + 1)
                        qb_ = state[(h, qblk)]
                        sb_ = t % 4
                        sy.op('pe', lambda e: e.matmul(ps_s[sb_][:, 0:n], lhsT=KTp[kp][:, kt * 128:(kt + 1) * 128],
                                                       rhs=QT[s_][qb_][:, 0:n], start=True, stop=(slot is None)),
                              r=[('KTp', kp), ('QT', s_, qb_)], w=[('ps_s', sb_)], sig=(slot is None))
                        if slot is not None:
                            sy.op('pe', lambda e: e.matmul(ps_s[sb_][:, 0:n], lhsT=ident_b[:],
                                                           rhs=TAB[hb_][:, slot, 0:n], start=False, stop=True),
                                  r=[('TAB', hb_), 'ident_b'], w=[('ps_s', sb_)])
                        sy.op('act', lambda e: e.activation(out=PB[sb_][:, 0:n], in_=ps_s[sb_][:, 0:n],
                                                            func=AF.Exp),
                              r=[('ps_s', sb_)], w=[('PB', sb_)])
                    u = t - LA
                    if 0 <= u < len(items):
                        h, qblk, ci, nk, kt, slot = items[u]
                        n = qblk[1]
                        hb_ = h % 2
                        if ci == 0:
                            state[('o', h, qblk)] = cnt['o'] % 2
                            cnt['o'] += 1
                        ob_ = state[('o', h, qblk)]
                        sb_ = u % 4
                        vi_ = ((h // 2) % 2) * 2 + (h % 2)
                        sy.op('pe', lambda e: e.matmul(ps_o[ob_][0:65, 0:n], lhsT=VP[vi_][:, kt, :],
                                                       rhs=PB[sb_][:, 0:n], start=(ci == 0), stop=(ci == nk - 1)),
                              r=[('VP', vi_), ('PB', sb_)], w=[('ps_o', ob_)], sig=True)
                        if ci == nk - 1:
                            epilogue(ob_, h, qblk)
                assert not pend
                sy.barrier()

        def phase_out(L, first, last, need_ctx):
            i = L // 2
            wsrc = ev_w_out if L % 2 == 0 else od_w_out
            with contextlib.ExitStack() as ps:
                wout = sbuf(ps, "wout", [128, 8, D], BF16)
                st = {'wi': 0}
                with contextlib.ExitStack() as wst:
                    stg = [sbuf(wst, "stgc%d" % b, [128, D], F32) for b in range(2)]
                    load_cast_weight(wst, wout, lambda k: wsrc[i, k * 128:(k + 1) * 128, :], D, 'wout', stg, st, 8)
                    sy.barrier()
                oTt = [sbuf(ps, "oTt%d" % b, [128, 4, 512], BF16) for b in range(2)]
                gTt = [sbuf(ps, "gTt%d" % b, [128, 4, 512], BF16) for b in range(2)]
                mix = [sbuf(ps, "mix%d" % b, [128, 4, 512], BF16) for b in range(2)]
                mixF = [sbuf(ps, "mixF%d" % b, [128, 8, 512], BF16) for b in range(2)]
                xt = [sbuf(ps, "xtc%d" % b, [128, D], F32) for b in range(2)]
                yt = [sbuf(ps, "ytc%d" % b, [128, D], F32) for b in range(2)]
                xn = [sbuf(ps, "xnc%d" % b, [128, D], F32) for b in range(2)]
                junk = sbuf(ps, "junkc", [128, D], BF16)
                ssq = [sbuf(ps, "ssqc%d" % b, [128, 1], F32) for b in range(2)]
                ot = [sbuf(ps, "otc%d" % b, [128, D], F32) for b in range(2)]
                ps_y = [psum(ps, "ps_y%d" % b, [128, 512]) for b in range(4)]
                cnt = {'y': 0, 'x': 0, 'm': 0, 'f': 0}
                oTv = oT_s.rearrange("(k p) t -> p k t", p=128)
                gTv = gT_s.rearrange("(k p) t -> p k t", p=128)

                def chunk_aps(ck):
                    if ck[0] < C:
                        return mixp_c, mixf_c
                    ci = (ck[0] - C) // 1024
                    return mixp[ci], mixf[ci]

                def stage_a(ck):
                    c0, cn = ck
                    mp, mf = chunk_aps(ck)
                    mpv = mp.rearrange("(k p) t -> p k t", p=128)
                    for r0 in range(c0, c0 + cn, 512):
                        n = min(512, c0 + cn - r0)
                        bb = cnt['m'] % 2
                        cnt['m'] += 1
                        sy.dma('sp', oTt[bb][:, :, 0:n], oTv[:, :, r0:r0 + n], w=[('oTt', bb)])
                        sy.dma('sp', gTt[bb][:, :, 0:n], gTv[:, :, r0:r0 + n], w=[('gTt', bb)])
                        sy.op('dve', lambda e: e.tensor_tensor(out=mix[bb][:, :, 0:n], in0=oTt[bb][:, :, 0:n],
                                                               in1=gTt[bb][:, :, 0:n], op=ALU.mult),
                              r=[('oTt', bb), ('gTt', bb)], w=[('mix', bb)])
                        sy.dma('pool', mpv[:, :, r0 - c0:r0 - c0 + n], mix[bb][:, :, 0:n], r=[('mix', bb)],
                               w=[('mp', c0)])
                    sy.allgather(mf, mp, r=[('mp', c0)], w=[('mf', c0)])

                def stage_b(ck):
                    c0, cn = ck
                    mp, mf = chunk_aps(ck)
                    mfv = mf.rearrange("(k p) t -> p k t", p=128)
                    for r0 in range(c0, c0 + cn, 512):
                        n = min(512, c0 + cn - r0)
                        fb = cnt['f'] % 2
                        cnt['f'] += 1
                        sy.dma('sp', mixF[fb][:, :, 0:n], mfv[:, :, r0 - c0:r0 - c0 + n], r=[('mf', c0)],
                               w=[('mixF', fb)])
                        for j in range(n // 128):
                            g = r0 // 128 + j
                            rr = "c" if g < 2 else "l"
                            xs = cnt['x'] % 2
                            cnt['x'] += 1
                            sy.dma('sp', xt[xs][:], x_rows(first, g * 128, 128), w=[('xt', xs)])
                            for hf in range(2):
                                b = cnt['y'] % 4
                                cnt['y'] += 1
                                for k in range(8):
                                    sy.op('pe', lambda e, k=k: e.matmul(
                                        ps_y[b][:, :], lhsT=mixF[fb][:, k, j * 128:(j + 1) * 128],
                                        rhs=wout[:, k, hf * 512:(hf + 1) * 512], start=(k == 0), stop=(k == 7)),
                                          r=[('mixF', fb), 'wout'], w=[('ps_y', b)], sig=(k == 7))
                                sy.op('dve', lambda e: e.tensor_tensor(out=yt[xs][:, hf * 512:(hf + 1) * 512],
                                                                       in0=ps_y[b][:, :],
                                                                       in1=modt[("G", rr)][:, hf * 512:(hf + 1) * 512],
                                                                       op=ALU.mult),
                                      r=[('ps_y', b), ('mod', 'G', rr)], w=[('yt', xs)])
                            sy.op('pool', lambda e: e.tensor_tensor(out=xn[xs][:], in0=yt[xs][:], in1=xt[xs][:],
                                                                    op=ALU.add),
                                  r=[('yt', xs), ('xt', xs)], w=[('xn', xs)])
                            if not (last and final):
                                dst = xdst if last else xcur
                                sy.dma('pool', dst[g * 128:(g + 1) * 128, :], xn[xs][:], r=[('xn', xs)])
                            else:
                                sy.op('act', lambda e: e.activation(out=junk[:], in_=xn[xs][:], func=AF.Square,
                                                                    accum_out=ssq[xs][:]),
                                      r=[('xn', xs)], w=['junk', ('ssq', xs)])
                                sy.op('act', lambda e: e.activation(out=ssq[xs][:], in_=ssq[xs][:], func=AF.Sqrt,
                                                                    scale=1.0 / D, bias=epsb[:, 0:1]),
                                      r=[('ssq', xs), 'epsb'], w=[('ssq', xs)])
                                sy.op('dve', lambda e: e.reciprocal(out=ssq[xs][:], in_=ssq[xs][:]),
                                      r=[('ssq', xs)], w=[('ssq', xs)])
                                sy.op('dve', lambda e: e.scalar_tensor_tensor(out=ot[xs][:], in0=xn[xs][:],
                                                                              scalar=ssq[xs][:, 0:1], in1=fbc[:],
                                                                              op0=ALU.mult, op1=ALU.mult),
                                      r=[('xn', xs), ('ssq', xs), 'fbc'], w=[('ot', xs)])
                                sy.dma('pool', xdst[(g - 2) * 128:(g - 1) * 128, :], ot[xs][:], r=[('ot', xs)])

                chunks = ([(0, C)] if need_ctx else []) + [(C + 1024 * c_, 1024) for c_ in range(S // 1024)]
                prev = None
                for ck in chunks:
                    stage_a(ck)
                    if prev is not None:
                        stage_b(prev)
                    prev = ck
                stage_b(prev)
                sy.barrier()

        sy.barrier()
        for li, L in enumerate(layers):
            first = (li == 0)
            last = (li == len(layers) - 1)
            need_ctx = L < DEPTH - 1
            phase_mod(L)
            if L % 2 == 0:
                phase_even(L, first, need_ctx)
            else:
                phase_odd(L, first, need_ctx)
            phase_out(L, first, last, need_ctx)
        sy.barrier()
        build.ninst = sy.ninst
    return nc


def _rope_tables():
    n_freq = 8
    inv = (10000.0 ** (-np.arange(n_freq, dtype=np.float32) / n_freq)).astype(np.float32)
    t = np.arange(S)
    ang_r = (t // GW).astype(np.float32)[:, None] * inv
    ang_c = (t % GW).astype(np.float32)[:, None] * inv
    cr, sr, cc, sc = np.cos(ang_r), np.sin(ang_r), np.cos(ang_c), np.sin(ang_c)
    cosT = np.concatenate([cr, cr, cc, cc], axis=1).T
    sinT = np.concatenate([-sr, sr, -sc, sc], axis=1).T
    return np.ascontiguousarray(cosT, np.float32), np.ascontiguousarray(sinT, np.float32)


_ROPE_PERM = np.concatenate([np.arange(8, 16), np.arange(0, 8), np.arange(24, 32), np.arange(16, 24)])


def _na_compact_tables(rpb):
    cols = np.arange(GW)
    cs = np.clip(cols - 8, 0, GW - 16)
    kc = np.arange(GW)[:, None]
    qc = np.arange(GW)[None, :]
    valid = (kc >= cs[None, :]) & (kc < cs[None, :] + 16)
    dc = np.clip(kc - qc + 15, 0, 30)
    out = np.full((rpb.shape[0], rpb.shape[1], GW, 15, GW), MASKV, np.float32)
    for e in range(15):
        dr_idx = (7 - e) + 7
        g = rpb[:, :, dr_idx, :][:, :, dc]
        out[:, :, :, e, :] = np.where(valid[None, None], g, np.float32(MASKV))
    return np.ascontiguousarray(out.reshape(rpb.shape[0], rpb.shape[1], GW, 15 * GW))


def _core_inputs(inp, p):
    f = lambda a: np.ascontiguousarray(a, np.float32)
    cosT, sinT = _rope_tables()
    w_uq = inp['mla_w_uq'].reshape(2, 256, 8, 96)
    w_uqp = w_uq.copy()
    w_uqp[..., 64:] = w_uq[..., 64:][..., _ROPE_PERM]
    sel = np.zeros((2, 2, 128), np.float32)
    sel[0, 0, :] = 1.0
    sel[1, 1, :] = 1.0
    w_in = inp['ev_w_in']
    ev_w_in = np.concatenate([w_in[:, :, 0:416], w_in[:, :, 416 + 256 * p:416 + 256 * p + 256],
                              w_in[:, :, 928 + 256 * p:928 + 256 * p + 256],
                              w_in[:, :, 1440 + 256 * p:1440 + 256 * p + 256]], axis=2)
    w_out = inp['ev_w_out']
    ev_w_out = np.concatenate([w_out[:, 0:256], w_out[:, 512:768], w_out[:, 256:512], w_out[:, 768:1024]], axis=1)
    ow = inp['od_w_in']
    od_w_in = np.concatenate([ow[:, :, sec * 1024 + 512 * p:sec * 1024 + 512 * p + 512] for sec in range(4)], axis=2)

    def per_part(a):
        return f(a.reshape(2, 2, 4, 128).transpose(0, 3, 1, 2)[:, :, :, 2 * p:2 * p + 2])

    return {
        'ada_w': f(inp['ada_w']), 'ada_b': f(inp['ada_b']), 'norm_g': f(inp['norm_g']),
        'ev_w_in': f(ev_w_in),
        'ev_w_krp': f(inp['ev_w_in'][:, :, 384:416][:, :, _ROPE_PERM]),
        'q_norm': f(inp['mla_q_norm'].reshape(2, 2, 128).transpose(0, 2, 1)),
        'kv_norm': f(inp['mla_kv_norm'].reshape(2, 128, 1)),
        'w_uq': f(w_uq[:, :, 4 * p:4 * p + 4].reshape(2, 256, 384)),
        'w_uqp': f(w_uqp[:, :, 4 * p:4 * p + 4].reshape(2, 256, 384)),
        'w_ukv': f(inp['mla_w_ukv'].reshape(2, 128, 8, 128)[:, :, 4 * p:4 * p + 4].reshape(2, 128, 512)),
        'conv_w': f(inp['lru_conv_w'].reshape(2, 4, 4, 128).transpose(0, 3, 2, 1)[:, :, 2 * p:2 * p + 2, :]),
        'conv_b': f(inp['lru_conv_b'].reshape(2, 4, 128).transpose(0, 2, 1)[:, :, 2 * p:2 * p + 2]),
        'lru_wa': f(inp['lru_wa'][:, :, 4 * p:4 * p + 4]), 'lru_wx': f(inp['lru_wx'][:, :, 4 * p:4 * p + 4]),
        'lru_ba': per_part(inp['lru_ba']), 'lru_bx': per_part(inp['lru_bx']),
        'lru_lam': per_part(inp['lru_lambda']),
        'ev_w_out': f(ev_w_out), 'od_w_in': f(od_w_in),
        'na_ct': f(_na_compact_tables(np.asarray(inp['na_rpb'], np.float32))[:, 8 * p:8 * p + 8]),
        'od_w_out': f(inp['od_w_out']),
        'fin_g': f(np.broadcast_to(inp['final_norm_g'].reshape(1, D), (128, D))),
        'ident': np.eye(128, dtype=np.float32), 'sel': sel, 'cosT': cosT, 'sinT': sinT,
    }


_NC_CACHE = {}


def _get_nc(layers, final):
    key = (tuple(layers), final)
    if key not in _NC_CACHE:
        _NC_CACHE[key] = build(list(layers), final)
    return _NC_CACHE[key]


FUSED = True


def kernel(**inp):
    inp = {k: np.asarray(v) for k, v in inp.items()}
    common = [_core_inputs(inp, p) for p in range(2)]
    xs = []
    cts = []
    for c_ in range(NCORES):
        b = c_ // 2
        xs.append(np.ascontiguousarray(np.concatenate([inp['ctx'][b], inp['x'][b]], axis=0), np.float32))
        cv = np.stack([inp['c'][b], inp['c_ctx']], axis=1)
        cts.append(np.ascontiguousarray(cv.reshape(8, 128, 2).transpose(1, 0, 2), np.float32))
    groups = [list(range(DEPTH))] if FUSED else [[l] for l in range(DEPTH)]
    for gi, layers in enumerate(groups):
        final = (gi == len(groups) - 1)
        nc = _get_nc(layers, final)
        in_maps = []
        for c_ in range(NCORES):
            m = dict(common[c_ % 2])
            m['xsrc'] = xs[c_]
            m['cT'] = cts[c_]
            in_maps.append(m)
        res = run_bass_kernel_spmd(nc, in_maps, core_ids=list(range(NCORES)))
        if final:
            out = np.stack([np.asarray(res.results[2 * b]['out'], np.float32) for b in range(B)], axis=0)
            return out
        xs = [np.ascontiguousarray(res.results[c_]['xdst'], np.float32) for c_ in range(NCORES)]
```

```python
import contextlib
import numpy as np
import concourse.bass as bass
import concourse.mybir as mybir
from concourse.bass_utils import run_bass_kernel_spmd

F32 = mybir.dt.float32
BF16 = mybir.dt.bfloat16
AF = mybir.ActivationFunctionType
ALU = mybir.AluOpType

D = 1024
B = 4
S = 8192
C = 256
T = C + S
DEPTH = 4
GW = 64
ROWS = S // GW
EPS = 1e-6
MLA_H = 4
MLA_SCALE = 96 ** -0.5
EVEN_IN = 1184
NA_H = 8
PAIRS = [[0, 1], [2, 3], [4, 5], [6, 7]]
MASKV = -30000.0
NCORES = 8

BLOCKS = [(0, C)] + [(C + 512 * i, 512) for i in range(S // 512)]


class Sync:
    def __init__(self, nc, es, ndma=10):
        self.nc = nc
        self.eng = {'pe': nc.tensor, 'dve': nc.vector, 'act': nc.scalar, 'pool': nc.gpsimd, 'sp': nc.sync}
        self.sem = {}
        self.cnt = {}
        for e in ('pe', 'dve', 'act', 'pool'):
            self.sem[('e', e)] = es.enter_context(nc.semaphore('s_' + e))
            self.cnt[('e', e)] = 0
        self.dq = {}
        for q in ('sp', 'pool', 'act'):
            keys = []
            for i in range(ndma):
                k = ('d', q, i)
                self.sem[k] = es.enter_context(nc.semaphore('d_%s%d' % (q, i)))
                self.cnt[k] = 0
                keys.append(k)
            self.dq[q] = [keys, 0]
        self.sem[('c',)] = es.enter_context(nc.semaphore('s_cc'))
        self.cnt[('c',)] = 0
        self.waited = {e: {} for e in self.eng}
        self.tok = {}
        self.pending = {e: False for e in self.eng}
        self.ninst = 0

    def _wait(self, e, ev):
        if ev is None:
            return
        k, v = ev
        if k == ('e', e) and e == 'pe':
            return
        if self.waited[e].get(k, 0) >= v:
            return
        self.eng[e].wait_ge(self.sem[k], v)
        self.waited[e][k] = v
        self.ninst += 1

    def _deps(self, e, r, w):
        for t in r:
            st = self.tok.get(t)
            if st is not None:
                self._wait(e, st[0])
        for t in w:
            st = self.tok.get(t)
            if st is not None:
                self._wait(e, st[0])
                for k, v in st[1].items():
                    self._wait(e, (k, v))

    def _record(self, ev, r, w):
        for t in r:
            st = self.tok.setdefault(t, [None, {}])
            if st[1].get(ev[0], 0) < ev[1]:
                st[1][ev[0]] = ev[1]
        for t in w:
            self.tok[t] = [ev, {}]

    def op(self, e, fn, r=(), w=(), sig=True):
        self._deps(e, r, w)
        inst = fn(self.eng[e])
        k = ('e', e)
        ev = (k, self.cnt[k] + 1)
        if sig:
            inst.then_inc(self.sem[k], 1)
            self.cnt[k] += 1
            self.pending[e] = False
        else:
            self.pending[e] = True
        self._record(ev, r, w)
        self.ninst += 1
        return inst

    def dma(self, q, out, in_, r=(), w=()):
        self._deps(q, r, w)
        keys, nxt = self.dq[q]
        k = keys[nxt]
        self.dq[q][1] = (nxt + 1) % len(keys)
        if self.cnt[k] > 0:
            self._wait(q, (k, 16 * self.cnt[k]))
        self.eng[q].dma_start(out=out, in_=in_).then_inc(self.sem[k], 16)
        self.cnt[k] += 1
        ev = (k, 16 * self.cnt[k])
        self._record(ev, r, w)
        self.ninst += 1

    def allreduce(self, out, in_, r=(), w=()):
        self._deps('pool', r, w)
        k = ('c',)
        inst = self.nc.gpsimd.collective_compute("AllReduce", ALU.add, replica_groups=PAIRS,
                                                 ins=[in_.opt()], outs=[out.opt()])
        inst.then_inc(self.sem[k], 1)
        self.cnt[k] += 1
        self._record((k, self.cnt[k]), r, w)
        self.ninst += 1

    def allgather(self, out, in_, r=(), w=()):
        self._deps('pool', r, w)
        k = ('c',)
        inst = self.nc.gpsimd.collective_compute("AllGather", ALU.bypass, replica_groups=PAIRS,
                                                 ins=[in_.opt()], outs=[out.opt()])
        inst.then_inc(self.sem[k], 1)
        self.cnt[k] += 1
        self._record((k, self.cnt[k]), r, w)
        self.ninst += 1

    def barrier(self, engines=('pe', 'dve', 'act', 'pool', 'sp')):
        for e in self.eng:
            assert not self.pending[e], e
        for e in engines:
            for k, c in self.cnt.items():
                if c == 0:
                    continue
                v = c * 16 if k[0] == 'd' else c
                self._wait(e, (k, v))
        self.tok = {}

    def drain_dma(self, e='sp'):
        for k, c in self.cnt.items():
            if k[0] == 'd' and c > 0:
                self._wait(e, (k, 16 * c))


def _valid_irange(qr0, kr):
    ok = []
    for i in range(8):
        r = qr0 + i
        rs = min(max(r - 4, 0), ROWS - 8)
        ok.append(rs <= kr < rs + 8)
    idx = [i for i in range(8) if ok[i]]
    if not idx:
        return None
    assert idx == list(range(idx[0], idx[-1] + 1))
    return idx[0], idx[-1] + 1


def _na_chunks(qb):
    qr0 = 8 * qb
    lo = max(qr0 - 4, 0)
    hi = min(qr0 + 7 + 3, ROWS - 1)
    return list(range(lo // 2, hi // 2 + 1))


def build(layers, final, debug=False):
    nc = bass.Bass("TRN2", target_bir_lowering=False)
    es = contextlib.ExitStack()

    def din(name, shape, dt=F32):
        return nc.dram_tensor(name, list(shape), dt, kind="ExternalInput").ap()

    def dout(name, shape, dt=F32):
        return nc.dram_tensor(name, list(shape), dt, kind="ExternalOutput").ap()

    def dscr(name, shape, dt):
        kind = "ExternalOutput" if debug else "Internal"
        return nc.dram_tensor(name, list(shape), dt, kind=kind).ap()

    xsrc = din("xsrc", [T, D])
    cT_d = din("cT", [128, 8, 2])
    ada_w = din("ada_w", [DEPTH, D, 3 * D])
    ada_b = din("ada_b", [DEPTH, 3 * D])
    norm_g = din("norm_g", [DEPTH, D])
    ev_w_in = din("ev_w_in", [2, D, EVEN_IN])
    ev_w_krp = din("ev_w_krp", [2, D, 32])
    q_norm = din("q_norm", [2, 128, 2])
    kv_norm = din("kv_norm", [2, 128, 1])
    w_uq = din("w_uq", [2, 256, 384])
    w_uqp = din("w_uqp", [2, 256, 384])
    w_ukv = din("w_ukv", [2, 128, 512])
    conv_w = din("conv_w", [2, 128, 2, 4])
    conv_b = din("conv_b", [2, 128, 2])
    lru_wa = din("lru_wa", [2, 2, 4, 64, 64])
    lru_wx = din("lru_wx", [2, 2, 4, 64, 64])
    lru_ba = din("lru_ba", [2, 128, 2, 2])
    lru_bx = din("lru_bx", [2, 128, 2, 2])
    lru_lam = din("lru_lam", [2, 128, 2, 2])
    ev_w_out = din("ev_w_out", [2, D, D])
    od_w_in = din("od_w_in", [2, D, 2048])
    na_ct = din("na_ct", [2, NA_H, 64, 15 * 64])
    od_w_out = din("od_w_out", [2, D, D])
    fin_g = din("fin_g", [128, D])
    ident_d = din("ident", [128, 128])
    sel_d = din("sel", [2, 2, 128])
    cos_d = din("cosT", [32, S])
    sin_d = din("sinT", [32, S])

    if final:
        xdst = dout("out", [S, D])
    else:
        xdst = dout("xdst", [T, D])
    xcur = dscr("xcur", [T, D], F32) if len(layers) > 1 else None
    gT_s = dscr("gT_s", [512, T], BF16)
    oT_s = dscr("oT_s", [512, T], BF16)
    xlT_s = dscr("xlT_s", [256, T], F32)
    NCHK = S // 1024
    mixp = nc.dram_tensor("mixp", [NCHK, 512, 1024], BF16, kind="Internal").ap()
    mixf = nc.dram_tensor("mixf", [NCHK, 1024, 1024], BF16, kind="Internal").ap()
    mixp_c = nc.dram_tensor("mixp_c", [512, C], BF16, kind="Internal").ap()
    mixf_c = nc.dram_tensor("mixf_c", [1024, C], BF16, kind="Internal").ap()
    qT_s = dscr("qT_s", [512, T], BF16)
    kT_s = dscr("kT_s", [512, T], BF16)
    vT_s = dscr("vT_s", [512, T], BF16)

    with es:
        sy = Sync(nc, es)

        uid = [0]

        def sbuf(stack, name, shape, dt):
            uid[0] += 1
            return stack.enter_context(nc.sbuf_tensor("%s_u%d" % (name, uid[0]), list(shape), dt))

        def psum(stack, name, shape, dt=F32):
            uid[0] += 1
            return stack.enter_context(nc.psum_tensor("%s_u%d" % (name, uid[0]), list(shape), dt))

        ident_f = sbuf(es, "ident_f", [128, 128], F32)
        ident_b = sbuf(es, "ident_b", [128, 128], BF16)
        ones_f = sbuf(es, "ones_f", [128, 128], F32)
        sel_sb = sbuf(es, "sel_sb", [2, 2, 128], F32)
        scT = sbuf(es, "scT", [128, 8, 2], F32)
        modt = {}
        for nm in ("A", "Bm", "G"):
            for rr in ("l", "c"):
                modt[(nm, rr)] = sbuf(es, "mod_%s_%s" % (nm, rr), [128, D], F32)
        fbc = sbuf(es, "fbc", [128, D], F32) if final else None

        sy.dma('sp', ident_f[:], ident_d, w=['ident_f'])
        sy.dma('sp', sel_sb[:], sel_d, w=['sel'])
        sy.dma('sp', scT[:], cT_d, w=['scT'])
        sy.op('dve', lambda e: e.tensor_copy(out=ident_b[:], in_=ident_f[:]), r=['ident_f'], w=['ident_b'])
        sy.op('dve', lambda e: e.memset(ones_f[:], 1.0), w=['ones_f'])
        epsb = sbuf(es, "epsb", [128, 1], F32)
        sy.op('dve', lambda e: e.memset(epsb[:], EPS), w=['epsb'])
        sy.op('act', lambda e: e.activation(out=scT[:], in_=scT[:], func=AF.Silu), r=['scT'], w=['scT'])
        if final:
            sy.dma('sp', fbc[:], fin_g, w=['fbc'])

        def x_rows(first, r0, n):
            if first:
                return xsrc[r0:r0 + n, :]
            return xcur[r0:r0 + n, :]

        def phase_mod(L):
            with contextlib.ExitStack() as ps:
                awb = [sbuf(ps, "awb%d" % i, [128, 8, 512], F32) for i in range(3)]
                mrow = sbuf(ps, "mrow", [2, 3 * D], F32)
                grow = sbuf(ps, "grow", [2, D], F32)
                arow = sbuf(ps, "arow", [2, D], F32)
                brow = sbuf(ps, "brow", [1, 3 * D], F32)
                ps_m = [psum(ps, "ps_m%d" % i, [2, 512]) for i in range(2)]
                ps_b = [psum(ps, "ps_b%d" % i, [128, 512]) for i in range(2)]
                sy.dma('sp', brow[:], ada_b[L:L + 1, :], w=['brow'])
                sy.dma('sp', grow[0:1, :], norm_g[L:L + 1, :], w=['grow0'])
                sy.dma('sp', grow[1:2, :], norm_g[L:L + 1, :], w=['grow1'])
                aw = ada_w[L].rearrange("(k p) n -> p k n", p=128)
                for nb in range(6):
                    a = awb[nb % 3]
                    sy.dma('sp' if nb % 2 == 0 else 'act', a[:], aw[:, :, nb * 512:(nb + 1) * 512],
                           w=[('awb', nb % 3)])
                    pm = ps_m[nb % 2]
                    for k in range(8):
                        sy.op('pe', lambda e, k=k: e.matmul(pm[:], lhsT=scT[:, k, :], rhs=a[:, k, :],
                                                        start=(k == 0), stop=False),
                              r=[('awb', nb % 3), 'scT'], w=[('ps_m', nb % 2)], sig=False)
                    sy.op('pe', lambda e: e.matmul(pm[:], lhsT=ones_f[0:1, 0:2],
                                                   rhs=brow[0:1, nb * 512:(nb + 1) * 512],
                                                   start=False, stop=True),
                          r=['brow', 'ones_f'], w=[('ps_m', nb % 2)])
                    sy.op('act', lambda e: e.copy(out=mrow[:, nb * 512:(nb + 1) * 512], in_=pm[:]),
                          r=[('ps_m', nb % 2)], w=['mrow'])
                sy.op('dve', lambda e: e.scalar_tensor_tensor(out=arow[:], in0=mrow[:, D:2 * D], scalar=1.0,
                                                              in1=grow[:], op0=ALU.add, op1=ALU.mult),
                      r=['mrow', 'grow0', 'grow1'], w=['arow'])
                i = 0
                for nm, src in (("A", arow[:]), ("Bm", mrow[:, 0:D]), ("G", mrow[:, 2 * D:3 * D])):
                    for ri, rr in enumerate(("l", "c")):
                        for half in range(2):
                            pb = ps_b[i % 2]
                            sy.op('pe', lambda e: e.matmul(pb[:], lhsT=sel_sb[:, ri, :],
                                                           rhs=src[:, half * 512:(half + 1) * 512],
                                                           start=True, stop=True),
                                  r=['sel', 'arow', 'mrow'], w=[('ps_b', i % 2)])
                            sy.op('act', lambda e: e.copy(out=modt[(nm, rr)][:, half * 512:(half + 1) * 512],
                                                          in_=pb[:]),
                                  r=[('ps_b', i % 2)], w=[('mod', nm, rr)])
                            i += 1
                sy.barrier()

        def make_hT(st, first, hT, hbuf, blk, ps_t, tiles):
            r0, ntok = blk
            xt, junk, ssq, rstd, t1, hb = tiles
            for j in range(ntok // 128):
                g = (r0 // 128) + j
                rr = "c" if g < 2 else "l"
                sl = st['xi'] % 2
                st['xi'] += 1
                sy.dma('sp', xt[sl][:], x_rows(first, g * 128, 128), w=[('xt', sl)])
                sy.op('act', lambda e: e.activation(out=junk[:], in_=xt[sl][:], func=AF.Square,
                                                    accum_out=ssq[sl][:]),
                      r=[('xt', sl)], w=['junk', ('ssq', sl)])
                sy.op('act', lambda e: e.activation(out=rstd[sl][:], in_=ssq[sl][:], func=AF.Sqrt,
                                                    scale=1.0 / D, bias=epsb[:, 0:1]),
                      r=[('ssq', sl), 'epsb'], w=[('rstd', sl)])
                sy.op('dve', lambda e: e.reciprocal(out=rstd[sl][:], in_=rstd[sl][:]),
                      r=[('rstd', sl)], w=[('rstd', sl)])
                sy.op('dve', lambda e: e.scalar_tensor_tensor(out=t1[sl][:], in0=xt[sl][:],
                                                              scalar=rstd[sl][:, 0:1],
                                                              in1=modt[("A", rr)][:], op0=ALU.mult,
                                                              op1=ALU.mult),
                      r=[('xt', sl), ('rstd', sl), ('mod', 'A', rr)], w=['t1'])
                sy.op('pool', lambda e: e.tensor_tensor(out=hb[sl][:], in0=t1[sl][:],
                                                        in1=modt[("Bm", rr)][:], op=ALU.add),
                      r=['t1', ('mod', 'Bm', rr)], w=[('hb', sl)])
                pt = ps_t[sl]
                for k in range(8):
                    sy.op('pe', lambda e, k=k: e.transpose(pt[:, k, :], hb[sl][:, k * 128:(k + 1) * 128],
                                                           ident_b[:]),
                          r=[('hb', sl), 'ident_b'], w=[('ps_t', sl)], sig=(k == 7))
                sy.op('act', lambda e: e.copy(out=hT[hbuf][:, :, j * 128:(j + 1) * 128], in_=pt[:]),
                      r=[('ps_t', sl)], w=[('hT', hbuf)])

        def hT_parts(st, first, hT, hbuf, blk, ps_t, tiles):
            r0, ntok = blk
            xt, junk, ssq, rstd, t1, hb = tiles
            parts = []
            for j in range(ntok // 128):
                g = (r0 // 128) + j
                rr = "c" if g < 2 else "l"
                sl = st['xi'] % 2
                st['xi'] += 1

                def part_a(g=g, rr=rr, sl=sl):
                    sy.dma('sp', xt[sl][:], x_rows(first, g * 128, 128), w=[('xt', sl)])
                    sy.op('act', lambda e: e.activation(out=junk[:], in_=xt[sl][:], func=AF.Square,
                                                        accum_out=ssq[sl][:]),
                          r=[('xt', sl)], w=['junk', ('ssq', sl)])
                    sy.op('act', lambda e: e.activation(out=rstd[sl][:], in_=ssq[sl][:], func=AF.Sqrt,
                                                        scale=1.0 / D, bias=epsb[:, 0:1]),
                          r=[('ssq', sl), 'epsb'], w=[('rstd', sl)])
                    sy.op('dve', lambda e: e.reciprocal(out=rstd[sl][:], in_=rstd[sl][:]),
                          r=[('rstd', sl)], w=[('rstd', sl)])
                    sy.op('dve', lambda e: e.scalar_tensor_tensor(out=t1[sl][:], in0=xt[sl][:],
                                                                  scalar=rstd[sl][:, 0:1],
                                                                  in1=modt[("A", rr)][:], op0=ALU.mult,
                                                                  op1=ALU.mult),
                          r=[('xt', sl), ('rstd', sl), ('mod', 'A', rr)], w=['t1'])
                    sy.op('pool', lambda e: e.tensor_tensor(out=hb[sl][:], in0=t1[sl][:],
                                                            in1=modt[("Bm", rr)][:], op=ALU.add),
                          r=['t1', ('mod', 'Bm', rr)], w=[('hb', sl)])

                def part_b(j=j, sl=sl):
                    pt = ps_t[sl]
                    for k in range(8):
                        sy.op('pe', lambda e, k=k: e.transpose(pt[:, k, :], hb[sl][:, k * 128:(k + 1) * 128],
                                                               ident_b[:]),
                              r=[('hb', sl), 'ident_b'], w=[('ps_t', sl)], sig=(k == 7))
                    sy.op('act', lambda e: e.copy(out=hT[hbuf][:, :, j * 128:(j + 1) * 128], in_=pt[:]),
                          r=[('ps_t', sl)], w=[('hT', hbuf)])
                parts.append((part_a, part_b))
            return parts

        def run_blocks(st, first, hT, ps_t, tiles, block_chunks):
            for a_, b_ in hT_parts(st, first, hT, 0, BLOCKS[0], ps_t, tiles):
                a_()
                b_()
            for bi, blk in enumerate(BLOCKS):
                hbuf = bi % 2
                nxt = hT_parts(st, first, hT, (bi + 1) % 2, BLOCKS[bi + 1], ps_t, tiles) if bi + 1 < len(BLOCKS) else []
                slots = []
                for j, (a_, b_) in enumerate(nxt):
                    slots.append([a_] if j == 0 else [nxt[j - 1][1], a_])
                if nxt:
                    slots.append([nxt[-1][1]])
                chunks = block_chunks(bi, blk, hbuf)
                nch, ns = len(chunks), len(slots)
                pos = {}
                for si in range(ns):
                    pos.setdefault((si * nch) // ns, []).append(si)
                for ci, chf in enumerate(chunks):
                    for si in pos.get(ci, []):
                        for f_ in slots[si]:
                            f_()
                    chf()

        def load_cast_weight(ps, dst, src_rows, ncols, tokname, stg, st, nk_=8):
            for k in range(nk_):
                sl = st['wi'] % 2
                st['wi'] += 1
                sy.dma('sp', stg[sl][:, 0:ncols], src_rows(k), w=[('stg', sl)])
                d = dst[:, k, :] if nk_ > 1 else dst[:]
                eng = 'dve' if sl == 0 else 'pool'
                sy.op(eng, lambda e: e.tensor_copy(out=d, in_=stg[sl][:, 0:ncols]),
                      r=[('stg', sl)], w=[tokname])

        def phase_even(L, first, need_ctx):
            i = L // 2
            with contextlib.ExitStack() as lay:
                cqnT = sbuf(lay, "cqnT", [128, 2, T], BF16)
                ckvnT = sbuf(lay, "ckvnT", [128, T], BF16)
                KT = [sbuf(lay, "KT0", [128, T], BF16)]
                with contextlib.ExitStack() as ps:
                    w_in = sbuf(ps, "w_in", [128, 8, EVEN_IN], BF16)
                    wkrp = sbuf(ps, "wkrp", [128, 8, 96], BF16)
                    gq = sbuf(ps, "gq", [128, 2], F32)
                    gkv = sbuf(ps, "gkv", [128, 1], F32)
                    st = {'xi': 0, 'wi': 0}
                    with contextlib.ExitStack() as wst:
                        stg = [sbuf(wst, "stg%d" % b, [128, EVEN_IN], F32) for b in range(2)]
                        load_cast_weight(wst, w_in,
                                         lambda k: ev_w_in[i, k * 128:(k + 1) * 128, :], EVEN_IN, 'w_in', stg, st)
                        sy.op('dve', lambda e: e.memset(wkrp[:], 0.0), w=['wkrp'])
                        for k in range(8):
                            sl = st['wi'] % 2
                            st['wi'] += 1
                            sy.dma('sp', stg[sl][:, 0:32], ev_w_krp[i, k * 128:(k + 1) * 128, :], w=[('stg', sl)])
                            sy.op('dve', lambda e, k=k: e.tensor_copy(out=wkrp[:, k, 64:96], in_=stg[sl][:, 0:32]),
                                  r=[('stg', sl)], w=['wkrp'])
                        sy.dma('sp', gq[:], q_norm[i], w=['gq'])
                        sy.dma('sp', gkv[:], kv_norm[i], w=['gkv'])
                        sy.barrier()
                    xt = [sbuf(ps, "xt%d" % b, [128, D], F32) for b in range(2)]
                    junk = sbuf(ps, "junk", [128, D], BF16)
                    ssq = [sbuf(ps, "ssq%d" % b, [128, 1], F32) for b in range(2)]
                    rstd = [sbuf(ps, "rstd%d" % b, [128, 1], F32) for b in range(2)]
                    t1s = sbuf(ps, "t1_0", [128, D], F32)
                    t1 = [t1s, t1s]
                    hb = [sbuf(ps, "hb%d" % b, [128, D], BF16) for b in range(2)]
                    hT = [sbuf(ps, "hT%d" % b, [128, 8, 512], BF16) for b in range(2)]
                    cqsb = sbuf(ps, "cqsb", [128, 3, 512], F32)
                    sqsb = sbuf(ps, "sqsb", [128, 2, 512], F32)
                    rsb = [sbuf(ps, "rsb%d" % b, [128, 512], F32) for b in range(2)]
                    cs = [sbuf(ps, "cs%d" % b, [128, 512], F32) for b in range(2)]
                    kt1 = sbuf(ps, "kt1", [128, 512], F32)
                    kt2 = sbuf(ps, "kt2", [128, 512], F32)
                    ob = [sbuf(ps, "ob%d" % b, [128, 512], BF16) for b in range(3)]
                    of = [sbuf(ps, "of%d" % b, [128, 512], F32) for b in range(2)]
                    ps_t = [psum(ps, "ps_t%d" % b, [128, 8, 128], BF16) for b in range(2)]
                    ps_a = [psum(ps, "ps_a%d" % b, [128, 512]) for b in range(4)]
                    ps_ss = [psum(ps, "ps_ss%d" % b, [128, 512]) for b in range(2)]
                    cnt = {'a': 0, 'ob': 0, 'of': 0, 'sq': 0}

                    def proj(hbuf, lhs_fn, m, n):
                        b = cnt['a'] % 4
                        cnt['a'] += 1
                        for k in range(8):
                            sy.op('pe', lambda e, k=k: e.matmul(ps_a[b][0:m, 0:n], lhsT=lhs_fn(k),
                                                                rhs=hT[hbuf][:, k, 0:n], start=(k == 0),
                                                                stop=(k == 7)),
                                  r=[('hT', hbuf), 'w_in', 'wkrp'], w=[('ps_a', b)], sig=(k == 7))
                        return b

                    def even_chunks(bi, blk, hbuf):
                        r0, n = blk
                        out = []

                        def norm_group(grp, chunks, gain, nfeat):
                            pss = ps_ss[grp]
                            for ci, cj in enumerate(chunks):
                                b = proj(hbuf, lambda k, cj=cj: w_in[:, k, cj * 128:(cj + 1) * 128], 128, n)
                                sq = cnt['sq'] % 2
                                cnt['sq'] += 1
                                sy.op('act', lambda e: e.copy(out=cqsb[:, cj, 0:n], in_=ps_a[b][:, 0:n]),
                                      r=[('ps_a', b)], w=[('cqsb', cj)])
                                sy.op('act', lambda e: e.activation(out=sqsb[:, sq, 0:n], in_=ps_a[b][:, 0:n],
                                                                    func=AF.Square),
                                      r=[('ps_a', b)], w=[('sqsb', sq)])
                                sy.op('pe', lambda e: e.matmul(pss[:, 0:n], lhsT=ones_f[:], rhs=sqsb[:, sq, 0:n],
                                                               start=(ci == 0), stop=(ci == len(chunks) - 1)),
                                      r=[('sqsb', sq), 'ones_f'], w=[('ps_ss', grp)], sig=True)
                            sy.op('act', lambda e: e.activation(out=rsb[grp][:, 0:n], in_=pss[:, 0:n],
                                                                func=AF.Sqrt, scale=1.0 / nfeat,
                                                                bias=epsb[:, 0:1]),
                                  r=[('ps_ss', grp), 'epsb'], w=[('rsb', grp)])
                            sy.op('dve', lambda e: e.reciprocal(out=rsb[grp][:, 0:n], in_=rsb[grp][:, 0:n]),
                                  r=[('rsb', grp)], w=[('rsb', grp)])
                            for ci, cj in enumerate(chunks):
                                dst = cqnT[:, cj, r0:r0 + n] if grp == 0 else ckvnT[:, r0:r0 + n]
                                sy.op('dve', lambda e: e.scalar_tensor_tensor(out=dst, in0=cqsb[:, cj, 0:n],
                                                                              scalar=gain[:, ci:ci + 1],
                                                                              in1=rsb[grp][:, 0:n],
                                                                              op0=ALU.mult, op1=ALU.mult),
                                      r=[('cqsb', cj), ('rsb', grp), 'gq', 'gkv'],
                                      w=['cqnT' if grp == 0 else 'ckvnT'])
                        out.append(lambda: norm_group(0, (0, 1), gq, 256.0))
                        out.append(lambda: norm_group(1, (2,), gkv, 128.0))

                        def kr_group():
                            b1 = proj(hbuf, lambda k: w_in[:, k, 320:416], 96, n)
                            if r0 < C:
                                sy.op('act', lambda e: e.copy(out=KT[0][64:96, r0:r0 + n], in_=ps_a[b1][64:96, 0:n]),
                                      r=[('ps_a', b1)], w=[('KTr', 0)])
                                return
                            b2 = proj(hbuf, lambda k: wkrp[:, k, :], 96, n)
                            sy.dma('sp', cs[0][64:96, 0:n], cos_d[:, r0 - C:r0 - C + n], w=[('cs', 0)])
                            sy.dma('sp', cs[1][64:96, 0:n], sin_d[:, r0 - C:r0 - C + n], w=[('cs', 1)])
                            sy.op('dve', lambda e: e.tensor_tensor(out=kt1[64:96, 0:n], in0=ps_a[b1][64:96, 0:n],
                                                                   in1=cs[0][64:96, 0:n], op=ALU.mult),
                                  r=[('ps_a', b1), ('cs', 0)], w=['kt1'])
                            sy.op('dve', lambda e: e.tensor_tensor(out=kt2[64:96, 0:n], in0=ps_a[b2][64:96, 0:n],
                                                                   in1=cs[1][64:96, 0:n], op=ALU.mult),
                                  r=[('ps_a', b2), ('cs', 1)], w=['kt2'])
                            sy.op('dve', lambda e: e.tensor_tensor(out=KT[0][64:96, r0:r0 + n],
                                                                   in0=kt1[64:96, 0:n], in1=kt2[64:96, 0:n],
                                                                   op=ALU.add),
                                  r=['kt1', 'kt2'], w=[('KTr', 0)])
                        out.append(kr_group)

                        def gate_chunk(col0, row0):
                            b = proj(hbuf, lambda k: w_in[:, k, col0:col0 + 128], 128, n)
                            o = cnt['ob'] % 3
                            cnt['ob'] += 1
                            sy.op('act', lambda e: e.activation(out=ob[o][:, 0:n], in_=ps_a[b][:, 0:n], func=AF.Silu),
                                  r=[('ps_a', b)], w=[('ob', o)])
                            sy.dma('pool', gT_s[row0:row0 + 128, r0:r0 + n], ob[o][:, 0:n], r=[('ob', o)])

                        def xl_chunk(col0, row0):
                            b = proj(hbuf, lambda k: w_in[:, k, col0:col0 + 128], 128, n)
                            o = cnt['of'] % 2
                            cnt['of'] += 1
                            sy.op('dve', lambda e: e.tensor_copy(out=of[o][:, 0:n], in_=ps_a[b][:, 0:n]),
                                  r=[('ps_a', b)], w=[('of', o)])
                            sy.dma('pool', xlT_s[row0:row0 + 128, r0:r0 + n], of[o][:, 0:n], r=[('of', o)])
                        for cj in range(2):
                            out.append(lambda cj=cj: gate_chunk(416 + cj * 128, cj * 128))
                            out.append(lambda cj=cj: gate_chunk(928 + cj * 128, 256 + cj * 128))
                            out.append(lambda cj=cj: xl_chunk(672 + cj * 128, cj * 128))
                        return out

                    run_blocks(st, first, hT, ps_t, (xt, junk, ssq, rstd, t1, hb), even_chunks)
                    sy.barrier()
                with contextlib.ExitStack() as ps:
                    wuq = sbuf(ps, "wuq", [128, 2, 384], BF16)
                    wuqp = sbuf(ps, "wuqp", [128, 2, 384], BF16)
                    wukv = sbuf(ps, "wukv", [128, 512], BF16)
                    st = {'wi': 0}
                    with contextlib.ExitStack() as wst:
                        stg = [sbuf(wst, "stgb%d" % b, [128, 1024], F32) for b in range(2)]
                        load_cast_weight(wst, wuq, lambda k: w_uq[i, k * 128:(k + 1) * 128, :], 384, 'wuq', stg, st, 2)
                        load_cast_weight(wst, wuqp, lambda k: w_uqp[i, k * 128:(k + 1) * 128, :], 384, 'wuq', stg, st, 2)
                        load_cast_weight(wst, wukv, lambda k: w_ukv[i], 512, 'wukv', stg, st, 1)
                        sy.barrier()
                    KT.append(sbuf(ps, "KT1", [128, T], BF16))
                    sy.op('pool', lambda e: e.tensor_copy(out=KT[1][64:96, :], in_=KT[0][64:96, :]),
                          r=[('KTr', 0)], w=[('KTr', 1)])
                    VP = [sbuf(ps, "VP%d" % b, [128, 66, 65], BF16) for b in range(2)]
                    for b in range(2):
                        sy.op('pool', lambda e: e.memset(VP[b][:, :, 0:1], 1.0), w=[('VP', b)])
                    QT = [sbuf(ps, "QT%d" % b, [128, 512], BF16) for b in range(2)]
                    PB = [sbuf(ps, "PB%d" % b, [128, 512], BF16) for b in range(4)]
                    osb = [sbuf(ps, "osb%d" % b, [128, 512], F32) for b in range(2)]
                    rrow = [sbuf(ps, "rrow%d" % b, [128, 512], F32) for b in range(2)]
                    for b in range(2):
                        sy.op('pool', lambda e: e.memset(rrow[b][:], 0.0), w=[('rrow', b)])
                    onb = [sbuf(ps, "onb%d" % b, [128, 512], BF16) for b in range(2)]
                    csq = [[sbuf(ps, "csq%d_%d" % (a, b), [128, 512], F32) for b in range(2)] for a in range(2)]
                    qt1 = sbuf(ps, "qt1", [128, 512], F32)
                    qt2 = sbuf(ps, "qt2", [128, 512], F32)
                    ps_s = [psum(ps, "ps_s%d" % b, [128, 512]) for b in range(4)]
                    ps_o = [psum(ps, "ps_o%d" % b, [128, 512]) for b in range(2)]
                    ps_x = [psum(ps, "ps_x%d" % b, [128, 512]) for b in range(2)]
                    cnt = {'x': 0, 'q': 0, 'it': 0, 'o': 0}

                    def px():
                        b = cnt['x'] % 2
                        cnt['x'] += 1
                        return b

                    def build_kv(h):
                        hb_ = h % 2
                        for (r0, n) in BLOCKS:
                            b = px()
                            sy.op('pe', lambda e: e.matmul(ps_x[b][0:64, 0:n], lhsT=wukv[:, h * 128:h * 128 + 64],
                                                           rhs=ckvnT[:, r0:r0 + n], start=True, stop=True),
                                  r=['wukv', 'ckvnT'], w=[('ps_x', b)])
                            sy.op('dve', lambda e: e.tensor_copy(out=KT[hb_][0:64, r0:r0 + n],
                                                                 in_=ps_x[b][0:64, 0:n]),
                                  r=[('ps_x', b)], w=[('KTn', hb_)])
                        for g0 in range(0, 66, 8):
                            ng = min(8, 66 - g0)
                            b = px()
                            for g in range(ng):
                                t0 = (g0 + g) * 128
                                sy.op('pe', lambda e, g=g, t0=t0: e.matmul(
                                    ps_x[b][:, g * 64:(g + 1) * 64], lhsT=ckvnT[:, t0:t0 + 128],
                                    rhs=wukv[:, h * 128 + 64:h * 128 + 128], start=True, stop=True),
                                      r=['wukv', 'ckvnT'], w=[('ps_x', b)], sig=(g == ng - 1))
                            sy.op('dve', lambda e: e.tensor_copy(
                                out=VP[hb_][:, g0:g0 + ng, 1:65],
                                in_=ps_x[b][:, 0:ng * 64].rearrange("p (g d) -> p g d", d=64)),
                                  r=[('ps_x', b)], w=[('VP', hb_)])

                    def build_q(h, qblk):
                        r0, n = qblk
                        qb_ = cnt['q'] % 2
                        cnt['q'] += 1
                        b1 = px()
                        for j in range(2):
                            sy.op('pe', lambda e, j=j: e.matmul(ps_x[b1][0:96, 0:n],
                                                                lhsT=wuq[:, j, h * 96:(h + 1) * 96],
                                                                rhs=cqnT[:, j, r0:r0 + n], start=(j == 0),
                                                                stop=(j == 1)),
                                  r=['wuq', 'cqnT'], w=[('ps_x', b1)], sig=(j == 1))
                        if r0 < C:
                            sy.op('dve', lambda e: e.tensor_copy(out=QT[qb_][0:96, 0:n], in_=ps_x[b1][0:96, 0:n]),
                                  r=[('ps_x', b1)], w=[('QT', qb_)])
                            return qb_
                        b2 = px()
                        for j in range(2):
                            sy.op('pe', lambda e, j=j: e.matmul(ps_x[b2][0:96, 0:n],
                                                                lhsT=wuqp[:, j, h * 96:(h + 1) * 96],
                                                                rhs=cqnT[:, j, r0:r0 + n], start=(j == 0),
                                                                stop=(j == 1)),
                                  r=['wuq', 'cqnT'], w=[('ps_x', b2)], sig=(j == 1))
                        sy.dma('sp', csq[0][qb_][64:96, 0:n], cos_d[:, r0 - C:r0 - C + n], w=[('csq', 0, qb_)])
                        sy.dma('sp', csq[1][qb_][64:96, 0:n], sin_d[:, r0 - C:r0 - C + n], w=[('csq', 1, qb_)])
                        sy.op('dve', lambda e: e.tensor_copy(out=QT[qb_][0:64, 0:n], in_=ps_x[b1][0:64, 0:n]),
                              r=[('ps_x', b1)], w=[('QT', qb_)])
                        sy.op('dve', lambda e: e.tensor_tensor(out=qt1[64:96, 0:n], in0=ps_x[b1][64:96, 0:n],
                                                               in1=csq[0][qb_][64:96, 0:n], op=ALU.mult),
                              r=[('ps_x', b1), ('csq', 0, qb_)], w=['qt1'])
                        sy.op('dve', lambda e: e.tensor_tensor(out=qt2[64:96, 0:n], in0=ps_x[b2][64:96, 0:n],
                                                               in1=csq[1][qb_][64:96, 0:n], op=ALU.mult),
                              r=[('ps_x', b2), ('csq', 1, qb_)], w=['qt2'])
                        sy.op('dve', lambda e: e.tensor_tensor(out=QT[qb_][64:96, 0:n], in0=qt1[64:96, 0:n],
                                                               in1=qt2[64:96, 0:n], op=ALU.add),
                              r=['qt1', 'qt2'], w=[('QT', qb_)])
                        return qb_

                    def epilogue(ob_, h, qblk):
                        r0, n = qblk
                        sy.op('act', lambda e: e.copy(out=osb[ob_][0:65, 0:n], in_=ps_o[ob_][0:65, 0:n]),
                              r=[('ps_o', ob_)], w=[('osb', ob_)])
                        sy.op('dve', lambda e: e.reciprocal(out=rrow[ob_][0:1, 0:n], in_=osb[ob_][0:1, 0:n]),
                              r=[('osb', ob_)], w=[('rrow', ob_)])

                        def stage2():
                            b = px()
                            sy.op('pe', lambda e: e.matmul(ps_x[b][0:65, 0:n], lhsT=ones_f[:, 0:65],
                                                           rhs=rrow[ob_][:, 0:n], start=True, stop=True),
                                  r=[('rrow', ob_), 'ones_f'], w=[('ps_x', b)])
                            sy.op('dve', lambda e: e.tensor_tensor(out=onb[ob_][0:65, 0:n],
                                                                   in0=osb[ob_][0:65, 0:n],
                                                                   in1=ps_x[b][0:65, 0:n], op=ALU.mult),
                                  r=[('osb', ob_), ('ps_x', b)], w=[('onb', ob_)])
                            sy.dma('pool', oT_s[h * 64:(h + 1) * 64, r0:r0 + n], onb[ob_][1:65, 0:n],
                                   r=[('onb', ob_)])
                        pend.append((tnow[0] + 3, stage2))

                    qblocks = ([BLOCKS[0]] if need_ctx else []) + BLOCKS[1:]
                    items = []
                    for h in range(MLA_H):
                        for qblk in qblocks:
                            nk = 2 if qblk[0] < C else 66
                            for c in range(nk):
                                items.append((h, qblk, c, nk))
                    LA = 3
                    state = {}
                    build_kv(0)
                    pend = []
                    tnow = [0]
                    for t in range(len(items) + LA + 4):
                        tnow[0] = t
                        while pend and pend[0][0] <= t:
                            pend.pop(0)[1]()
                        if t < len(items):
                            h, qblk, c, nk = items[t]
                            n = qblk[1]
                            hb_ = h % 2
                            if c == 0:
                                if t == 0:
                                    state[(h, qblk)] = build_q(h, qblk)
                                if t + nk < len(items):
                                    h2, qblk2 = items[t + nk][0], items[t + nk][1]
                                    state[(h2, qblk2)] = build_q(h2, qblk2)
                                if qblk == qblocks[1 if len(qblocks) > 1 else 0] and h + 1 < MLA_H:
                                    build_kv(h + 1)
                            qb_ = state[(h, qblk)]
                            sb_ = t % 4
                            sy.op('pe', lambda e: e.matmul(ps_s[sb_][:, 0:n], lhsT=KT[hb_][0:96, c * 128:(c + 1) * 128],
                                                           rhs=QT[qb_][0:96, 0:n], start=True, stop=True),
                                  r=[('KTn', hb_), ('KTr', hb_), ('QT', qb_)], w=[('ps_s', sb_)])
                            sy.op('act', lambda e: e.activation(out=PB[sb_][:, 0:n], in_=ps_s[sb_][:, 0:n],
                                                                func=AF.Exp, scale=MLA_SCALE),
                                  r=[('ps_s', sb_)], w=[('PB', sb_)])
                        u = t - LA
                        if 0 <= u < len(items):
                            h, qblk, c, nk = items[u]
                            n = qblk[1]
                            hb_ = h % 2
                            if c == 0:
                                state[('o', h, qblk)] = cnt['o'] % 2
                                cnt['o'] += 1
                            ob_ = state[('o', h, qblk)]
                            sb_ = u % 4
                            sy.op('pe', lambda e: e.matmul(ps_o[ob_][0:65, 0:n], lhsT=VP[hb_][:, c, :],
                                                           rhs=PB[sb_][:, 0:n], start=(c == 0), stop=(c == nk - 1)),
                                  r=[('VP', hb_), ('PB', sb_)], w=[('ps_o', ob_)], sig=True)
                            if c == nk - 1:
                                epilogue(ob_, h, qblk)
                    assert not pend
                    sy.barrier()
            with contextlib.ExitStack() as ps:
                XL = sbuf(ps, "XL", [128, T], F32)
                U = sbuf(ps, "U", [128, T], F32)
                HS = sbuf(ps, "HS", [128, T], F32)
                OB = sbuf(ps, "OB", [128, T], BF16)
                PN = 1024
                Rt = [sbuf(ps, "Rt%d" % b, [128, PN], F32) for b in range(2)]
                It = [sbuf(ps, "It%d" % b, [128, PN], F32) for b in range(2)]
                At = [sbuf(ps, "At%d" % b, [128, PN], F32) for b in range(2)]
                Mt = [sbuf(ps, "Mt%d" % b, [128, PN], F32) for b in range(2)]
                Hp = [sbuf(ps, "Hp%d" % b, [128, PN], F32) for b in range(2)]
                carry = sbuf(ps, "carry", [128, 2], F32)
                wbd = [[sbuf(ps, "wbd%d_%d" % (d, g), [128, 128], F32) for g in range(2)] for d in range(2)]
                cw = sbuf(ps, "cw", [128, 2, 4], F32)
                cb = sbuf(ps, "cb", [128, 2], F32)
                bat = sbuf(ps, "bat", [128, 2, 2], F32)
                bxt = sbuf(ps, "bxt", [128, 2, 2], F32)
                lamt = sbuf(ps, "lamt", [128, 2, 2], F32)
                clt = sbuf(ps, "clt", [128, 2, 2], F32)
                cl2t = sbuf(ps, "cl2t", [128, 2, 2], F32)
                ps_r = [psum(ps, "ps_r%d" % b, [128, PN]) for b in range(2)]
                ps_i = [psum(ps, "ps_i%d" % b, [128, PN]) for b in range(2)]
                sy.dma('sp', cw[:], conv_w[i], w=['cw'])
                sy.dma('sp', cb[:], conv_b[i], w=['cb'])
                sy.dma('sp', bat[:], lru_ba[i], w=['bat'])
                sy.dma('sp', bxt[:], lru_bx[i], w=['bxt'])
                sy.dma('sp', lamt[:], lru_lam[i], w=['lamt'])
                sy.op('act', lambda e: e.activation(out=clt[:], in_=lamt[:], func=AF.Exp, scale=-1.0),
                      r=['lamt'], w=['clt'])
                sy.op('act', lambda e: e.activation(out=clt[:], in_=clt[:], func=AF.Ln, bias=1.0),
                      r=['clt'], w=['clt'])
                sy.op('dve', lambda e: e.tensor_scalar(out=cl2t[:], in0=clt[:], scalar1=-16.0, scalar2=None,
                                                       op0=ALU.mult), r=['clt'], w=['cl2t'])
                sy.op('dve', lambda e: e.tensor_scalar(out=clt[:], in0=clt[:], scalar1=-8.0, scalar2=None,
                                                       op0=ALU.mult), r=['clt', 'cl2t'], w=['clt'])
                pieces = [(0, C)] + [(C + PN * k, PN) for k in range(S // PN)]
                def rv(t_, n_):
                    full = t_[:]
                    return bass.AP(t_, n_ - 1, [[full.ap[0][0], 128], [-1, n_]])

                def prologue(cc):
                    sy.dma('sp', XL[:], xlT_s[cc * 128:(cc + 1) * 128, :], w=['XL'])
                    for d in range(2):
                        for g, wsrc in enumerate((lru_wa, lru_wx)):
                            sy.op('dve', lambda e: e.memset(wbd[d][g][:], 0.0), w=[('wbd', d, g)])
                            for hf in range(2):
                                sy.dma('sp', wbd[d][g][hf * 64:(hf + 1) * 64, hf * 64:(hf + 1) * 64],
                                       wsrc[i, d, cc * 2 + hf], w=[('wbd', d, g)])
                    for (s0, e0) in ((0, C), (C, T)):
                        sy.op('dve', lambda e: e.tensor_scalar(out=U[:, s0:e0], in0=XL[:, s0:e0],
                                                               scalar1=cw[:, cc, 2:3], scalar2=cb[:, cc:cc + 1],
                                                               op0=ALU.mult, op1=ALU.add),
                              r=['XL', 'cw', 'cb'], w=['U'])
                        for j, off in ((0, -2), (1, -1), (3, 1)):
                            if off < 0:
                                o_sl = (s0 - off, e0)
                                i_sl = (s0, e0 + off)
                            else:
                                o_sl = (s0, e0 - off)
                                i_sl = (s0 + off, e0)
                            sy.op('dve', lambda e: e.scalar_tensor_tensor(
                                out=U[:, o_sl[0]:o_sl[1]], in0=XL[:, i_sl[0]:i_sl[1]], scalar=cw[:, cc, j:j + 1],
                                in1=U[:, o_sl[0]:o_sl[1]], op0=ALU.mult, op1=ALU.add),
                                  r=['XL', 'cw', 'U'], w=['U'])

                def stage1(cc, d, pi, p0, pn, sl):
                    for q0 in range(0, pn, 512):
                        qn = min(512, pn - q0)
                        for g, pst in ((0, ps_r), (1, ps_i)):
                            sy.op('pe', lambda e: e.matmul(pst[sl][:, q0:q0 + qn], lhsT=wbd[d][g][:],
                                                           rhs=U[:, p0 + q0:p0 + q0 + qn], start=True, stop=True),
                                  r=['U', ('wbd', d, g)], w=[('psg', g, sl)])
                    sy.op('act', lambda e: e.activation(out=Rt[sl][:, 0:pn], in_=ps_r[sl][:, 0:pn],
                                                        func=AF.Sigmoid, bias=bat[:, d, cc:cc + 1]),
                          r=[('psg', 0, sl), 'bat'], w=[('Rt', sl)])
                    sy.op('act', lambda e: e.activation(out=It[sl][:, 0:pn], in_=ps_i[sl][:, 0:pn],
                                                        func=AF.Sigmoid, bias=bxt[:, d, cc:cc + 1]),
                          r=[('psg', 1, sl), 'bxt'], w=[('It', sl)])
                    sy.op('act', lambda e: e.activation(out=At[sl][:, 0:pn], in_=Rt[sl][:, 0:pn],
                                                        func=AF.Exp, scale=clt[:, d, cc:cc + 1]),
                          r=[('Rt', sl), 'clt'], w=[('At', sl)])
                    sy.op('act', lambda e: e.activation(out=Mt[sl][:, 0:pn], in_=Rt[sl][:, 0:pn],
                                                        func=AF.Exp, scale=cl2t[:, d, cc:cc + 1]),
                          r=[('Rt', sl), 'cl2t'], w=[('Mt', sl)])
                    sy.op('dve', lambda e: e.tensor_scalar(out=Mt[sl][:, 0:pn], in0=Mt[sl][:, 0:pn],
                                                           scalar1=-1.0, scalar2=1.0, op0=ALU.mult, op1=ALU.add),
                          r=[('Mt', sl)], w=[('Mt', sl)])
                    sy.op('dve', lambda e: e.tensor_tensor(out=It[sl][:, 0:pn], in0=It[sl][:, 0:pn],
                                                           in1=U[:, p0:p0 + pn], op=ALU.mult),
                          r=[('It', sl), 'U'], w=[('It', sl)])

                def stage2(cc, d, pi, p0, pn, sl):
                    sy.op('act', lambda e: e.activation(out=Mt[sl][:, 0:pn], in_=Mt[sl][:, 0:pn], func=AF.Sqrt),
                          r=[('Mt', sl)], w=[('Mt', sl)])
                    sy.op('dve', lambda e: e.tensor_tensor(out=It[sl][:, 0:pn], in0=It[sl][:, 0:pn],
                                                           in1=Mt[sl][:, 0:pn], op=ALU.mult),
                          r=[('It', sl), ('Mt', sl)], w=[('It', sl)])
                    init = 0.0 if pi == 0 else carry[:, d:d + 1]
                    if d == 0:
                        sy.op('dve', lambda e: e.tensor_tensor_scan(
                            out=HS[:, p0:p0 + pn], data0=At[sl][:, 0:pn], data1=It[sl][:, 0:pn],
                            initial=init, op0=ALU.mult, op1=ALU.add),
                              r=[('At', sl), ('It', sl), 'carry'], w=['HS'])
                        sy.op('dve', lambda e: e.tensor_copy(out=carry[:, 0:1], in_=HS[:, p0 + pn - 1:p0 + pn]),
                              r=['HS'], w=['carry'])
                    else:
                        sy.op('dve', lambda e: e.tensor_tensor_scan(
                            out=rv(Hp[sl], pn), data0=rv(At[sl], pn), data1=rv(It[sl], pn),
                            initial=init, op0=ALU.mult, op1=ALU.add),
                              r=[('At', sl), ('It', sl), 'carry'], w=[('Hp', sl)])
                        sy.op('dve', lambda e: e.tensor_copy(out=carry[:, 1:2], in_=Hp[sl][:, 0:1]),
                              r=[('Hp', sl)], w=['carry'])
                        sy.op('pool', lambda e: e.tensor_tensor(out=OB[:, p0:p0 + pn], in0=HS[:, p0:p0 + pn],
                                                                in1=Hp[sl][:, 0:pn], op=ALU.add),
                              r=['HS', ('Hp', sl)], w=['OB'])
                    if d == 1 and pi == len(pieces) - 1:
                        sy.dma('pool', oT_s[256 + cc * 128:256 + (cc + 1) * 128, :], OB[:], r=['OB'])

                work = []
                for cc in range(2):
                    for d in range(2):
                        order = pieces if d == 0 else [pieces[0]] + pieces[:0:-1]
                        for pi, (p0, pn) in enumerate(order):
                            work.append((cc, d, pi, p0, pn, len(work) % 2))
                for wi in range(len(work) + 1):
                    if wi < len(work):
                        if work[wi][1] == 0 and work[wi][2] == 0:
                            prologue(work[wi][0])
                        stage1(*work[wi])
                    if wi >= 1:
                        stage2(*work[wi - 1])
                sy.barrier()

        def phase_odd(L, first, need_ctx):
            i = L // 2
            with contextlib.ExitStack() as ps:
                w_in = sbuf(ps, "w_ino", [128, 8, 2048], BF16)
                st = {'xi': 0, 'wi': 0}
                with contextlib.ExitStack() as wst:
                    stg = [sbuf(wst, "stgo%d" % b, [128, 2048], F32) for b in range(2)]
                    for hf in range(1):
                        for k in range(8):
                            sl = st['wi'] % 2
                            st['wi'] += 1
                            sy.dma('sp', stg[sl][:], od_w_in[i, k * 128:(k + 1) * 128, hf * 2048:(hf + 1) * 2048],
                                   w=[('stg', sl)])
                            sy.op('dve' if sl == 0 else 'pool',
                                  lambda e, k=k: e.tensor_copy(out=w_in[:, k, hf * 2048:(hf + 1) * 2048],
                                                               in_=stg[sl][:]),
                                  r=[('stg', sl)], w=['w_in'])
                    sy.barrier()
                xt = [sbuf(ps, "xto%d" % b, [128, D], F32) for b in range(2)]
                junk = sbuf(ps, "junko", [128, D], BF16)
                ssq = [sbuf(ps, "ssqo%d" % b, [128, 1], F32) for b in range(2)]
                rstd = [sbuf(ps, "rstdo%d" % b, [128, 1], F32) for b in range(2)]
                t1s = sbuf(ps, "t1o_0", [128, D], F32)
                t1 = [t1s, t1s]
                hb = [sbuf(ps, "hbo%d" % b, [128, D], BF16) for b in range(2)]
                hT = [sbuf(ps, "hTo%d" % b, [128, 8, 512], BF16) for b in range(2)]
                ob = [sbuf(ps, "obo%d" % b, [128, 512], BF16) for b in range(4)]
                ps_t = [psum(ps, "ps_to%d" % b, [128, 8, 128], BF16) for b in range(2)]
                ps_a = [psum(ps, "ps_ao%d" % b, [128, 512]) for b in range(4)]
                cnt = {'a': 0, 'ob': 0, 'vb': 0}
                def odd_chunks(bi, blk, hbuf):
                    r0, n = blk
                    out = []

                    def one(which, col0, dst, cj):
                        b = cnt['a'] % 4
                        cnt['a'] += 1
                        for k in range(8):
                            sy.op('pe', lambda e, k=k: e.matmul(
                                ps_a[b][:, 0:n], lhsT=w_in[:, k, col0 + cj * 128:col0 + (cj + 1) * 128],
                                rhs=hT[hbuf][:, k, 0:n], start=(k == 0), stop=(k == 7)),
                                  r=[('hT', hbuf), 'w_in'], w=[('ps_a', b)], sig=(k == 7))
                        o = cnt['ob'] % 4
                        cnt['ob'] += 1
                        if which == "q":
                            sy.op('dve', lambda e: e.tensor_scalar(out=ob[o][:, 0:n], in0=ps_a[b][:, 0:n],
                                                                   scalar1=0.125, scalar2=None, op0=ALU.mult),
                                  r=[('ps_a', b)], w=[('ob', o)])
                        elif which == "k":
                            sy.op('dve', lambda e: e.tensor_copy(out=ob[o][:, 0:n], in_=ps_a[b][:, 0:n]),
                                  r=[('ps_a', b)], w=[('ob', o)])
                        elif which == "v":
                            sy.op('act', lambda e: e.copy(out=ob[o][:, 0:n], in_=ps_a[b][:, 0:n]),
                                  r=[('ps_a', b)], w=[('ob', o)])
                        else:
                            sy.op('act', lambda e: e.activation(out=ob[o][:, 0:n], in_=ps_a[b][:, 0:n],
                                                                func=AF.Silu),
                                  r=[('ps_a', b)], w=[('ob', o)])
                        sy.dma('pool', dst[cj * 128:(cj + 1) * 128, r0:r0 + n], ob[o][:, 0:n], r=[('ob', o)])
                    for cj in range(4):
                        for which, col0, dst in (("q", 0, qT_s), ("k", 512, kT_s), ("v", 1024, vT_s),
                                                 ("g", 1536, gT_s)):
                            out.append(lambda which=which, col0=col0, dst=dst, cj=cj: one(which, col0, dst, cj))
                    return out

                run_blocks(st, first, hT, ps_t, (xt, junk, ssq, rstd, t1, hb), odd_chunks)
                sy.barrier()
            with contextlib.ExitStack() as ps:
                KTp = [sbuf(ps, "KTp%d" % b, [128, T], BF16) for b in range(2)]
                VTp = sbuf(ps, "VTp", [128, T], BF16)
                VP = [sbuf(ps, "VPn%d" % b, [128, 66, 65], BF16) for b in range(4)]
                CT = [sbuf(ps, "CT%d" % b, [128, 15 * 64], F32) for b in range(2)]
                TAB = [sbuf(ps, "TAB%d" % b, [128, 20, 512], BF16) for b in range(2)]
                QT = [[sbuf(ps, "QTn%d_%d" % (s_, b), [128, 512], BF16) for b in range(2)] for s_ in range(2)]
                PB = [sbuf(ps, "PBn%d" % b, [128, 512], BF16) for b in range(4)]
                osb = [sbuf(ps, "osbn%d" % b, [128, 512], F32) for b in range(2)]
                rrow = [sbuf(ps, "rrown%d" % b, [128, 512], F32) for b in range(2)]
                for b in range(2):
                    sy.op('pool', lambda e: e.memset(rrow[b][:], 0.0), w=[('rrow', b)])
                onb = [sbuf(ps, "onbn%d" % b, [128, 512], BF16) for b in range(2)]
                ps_s = [psum(ps, "ps_sn%d" % b, [128, 512]) for b in range(4)]
                ps_o = [psum(ps, "ps_on%d" % b, [128, 512]) for b in range(2)]
                ps_x = [psum(ps, "ps_xn0", [128, 512])]
                ps_tr = psum(ps, "ps_trn", [128, 8, 128], BF16)
                cnt = {'x': 0, 'q': 0, 'o': 0}
                for b in range(4):
                    sy.op('pool', lambda e: e.memset(VP[b][:, :, 0:1], 1.0), w=[('VP', b)])
                for b in range(2):
                    sy.op('pool', lambda e: e.memset(TAB[b][:], MASKV), w=[('TAB', b)])
                    for s_ in range(2):
                        sy.op('pool', lambda e: e.memset(QT[s_][b][:], 0.0), w=[('QT', s_, b)])
                slot_of = {}
                slots = []
                for typ, qb in ((0, 0), (1, 5), (2, 15)):
                    for ci, ch in enumerate(_na_chunks(qb)):
                        slot_of[(typ, ci)] = len(slots)
                        slots.append((qb, ch))
                assert len(slots) == 20

                def prep_dma(h):
                    hb_ = h % 2
                    if h % 2 == 0:
                        hp = h // 2
                        sy.dma('sp', KTp[hp % 2][:], kT_s[hp * 128:(hp + 1) * 128, :], w=[('KTp', hp % 2)])
                        sy.dma('sp', VTp[:], vT_s[hp * 128:(hp + 1) * 128, :], w=['VTp'])
                    for hf in range(2):
                        sy.dma('sp', CT[hb_][hf * 64:(hf + 1) * 64, :], na_ct[i, h], w=[('CT', hb_)])

                def prep_head(h):
                    hb_ = h % 2
                    if h % 2 == 0:
                        hp = h // 2
                        pb_ = (h // 2) % 2
                        for g0 in range(0, 66, 8):
                            ng = min(8, 66 - g0)
                            for g in range(ng):
                                t0 = (g0 + g) * 128
                                sy.op('pe', lambda e, g=g, t0=t0: e.transpose(ps_tr[:, g, :], VTp[:, t0:t0 + 128],
                                                                              ident_b[:]),
                                      r=['VTp', 'ident_b'], w=['ps_tr'], sig=(g == ng - 1))
                            for s2 in range(2):
                                sy.op('dve', lambda e: e.tensor_copy(out=VP[pb_ * 2 + s2][:, g0:g0 + ng, 1:65],
                                                                     in_=ps_tr[:, 0:ng, s2 * 64:(s2 + 1) * 64]),
                                      r=['ps_tr'], w=[('VP', pb_ * 2 + s2)])
                    for si, (qb, ch) in enumerate(slots):
                        for a in range(2):
                            kr = 2 * ch + a
                            vr = _valid_irange(8 * qb, kr)
                            if vr is None:
                                continue
                            i0, i1 = vr
                            e0 = 7 - kr + 8 * qb + i0
                            assert 0 <= e0 and e0 + (i1 - i0) <= 15
                            sy.op('dve', lambda e: e.tensor_copy(
                                out=TAB[hb_][a * 64:(a + 1) * 64, si, i0 * 64:i1 * 64],
                                in_=CT[hb_][a * 64:(a + 1) * 64, e0 * 64:(e0 + i1 - i0) * 64]),
                                  r=[('CT', hb_)], w=[('TAB', hb_)])

                def px():
                    return 0

                def epilogue(ob_, h, qblk):
                    r0, n = qblk
                    sy.op('act', lambda e: e.copy(out=osb[ob_][0:65, 0:n], in_=ps_o[ob_][0:65, 0:n]),
                          r=[('ps_o', ob_)], w=[('osb', ob_)])
                    sy.op('dve', lambda e: e.reciprocal(out=rrow[ob_][0:1, 0:n], in_=osb[ob_][0:1, 0:n]),
                          r=[('osb', ob_)], w=[('rrow', ob_)])

                    def stage2():
                        b = px()
                        sy.op('pe', lambda e: e.matmul(ps_x[b][0:65, 0:n], lhsT=ones_f[:, 0:65],
                                                       rhs=rrow[ob_][:, 0:n], start=True, stop=True),
                              r=[('rrow', ob_), 'ones_f'], w=[('ps_x', b)])
                        sy.op('dve', lambda e: e.tensor_tensor(out=onb[ob_][0:65, 0:n], in0=osb[ob_][0:65, 0:n],
                                                               in1=ps_x[b][0:65, 0:n], op=ALU.mult),
                              r=[('osb', ob_), ('ps_x', b)], w=[('onb', ob_)])
                        sy.dma('pool', oT_s[h * 64:(h + 1) * 64, r0:r0 + n], onb[ob_][1:65, 0:n],
                               r=[('onb', ob_)])
                    pend.append((tnow[0] + 3, stage2))

                qblocks = ([BLOCKS[0]] if need_ctx else []) + BLOCKS[1:]
                items = []
                for h in range(NA_H):
                    for qblk in qblocks:
                        if qblk[0] < C:
                            chs = [(0, None), (1, None)]
                        else:
                            qb = (qblk[0] - C) // 512
                            typ = 0 if qb == 0 else (2 if qb == 15 else 1)
                            chs = [(0, None), (1, None)] + [(2 + ch, slot_of[(typ, ci)])
                                                            for ci, ch in enumerate(_na_chunks(qb))]
                        for ci, (kt, slot) in enumerate(chs):
                            items.append((h, qblk, ci, len(chs), kt, slot))
                LA = 3
                state = {}
                prep_dma(0)
                prep_head(0)
                pend = []
                tnow = [0]
                for t in range(len(items) + LA + 4):
                    tnow[0] = t
                    while pend and pend[0][0] <= t:
                        pend.pop(0)[1]()
                    if t < len(items):
                        h, qblk, ci, nk, kt, slot = items[t]
                        r0, n = qblk
                        hb_ = h % 2
                        s_ = h % 2
                        kp = (h // 2) % 2
                        if ci == 0:
                            def load_q(h2, qblk2):
                                q2 = cnt['q'] % 2
                                cnt['q'] += 1
                                state[(h2, qblk2)] = q2
                                s2_ = h2 % 2
                                sy.dma('sp', QT[s2_][q2][s2_ * 64:(s2_ + 1) * 64, 0:qblk2[1]],
                                       qT_s[h2 * 64:(h2 + 1) * 64, qblk2[0]:qblk2[0] + qblk2[1]],
                                       w=[('QT', s2_, q2)])
                            if t == 0:
                                load_q(h, qblk)
                            if t + nk < len(items):
                                load_q(items[t + nk][0], items[t + nk][1])
                            if qblk == qblocks[1 if len(qblocks) > 1 else 0] and h + 1 < NA_H:
                                prep_dma(h + 1)
                            if qblk == qblocks[min(9, len(qblocks) - 1)] and h + 1 < NA_H:
                                prep_head(h + 1)
                        qb_ = state[(h, qblk)]
                        sb_ = t % 4
                        sy.op('pe', lambda e: e.matmul(ps_s[sb_][:, 0:n], lhsT=KTp[kp][:, kt * 128:(kt + 1) * 128],
                                                       rhs=QT[s_][qb_][:, 0:n], start=True, stop=(slot is None)),
                              r=[('KTp', kp), ('QT', s_, qb_)], w=[('ps_s', sb_)], sig=(slot is None))
                        if slot is not None:
                            sy.op('pe', lambda e: e.matmul(ps_s[sb_][:, 0:n], lhsT=ident_b[:],
                                                           rhs=TAB[hb_][:, slot, 0:n], start=False, stop=True),
                                  r=[('TAB', hb_), 'ident_b'], w=[('ps_s', sb_)])
                        sy.op('act', lambda e: e.activation(out=PB[sb_][:, 0:n], in_=ps_s[sb_][:, 0:n],
                                                            func=AF.Exp),
                              r=[('ps_s', sb_)], w=[('PB', sb_)])
                    u = t - LA
                    if 0 <= u < len(items):
                        h, qblk, ci, nk, kt, slot = items[u]
                        n = qblk[1]
                        hb_ = h % 2
                        if ci == 0:
                            state[('o', h, qblk)] = cnt['o'] % 2
                            cnt['o'] += 1
                        ob_ = state[('o', h, qblk)]
                        sb_ = u % 4
                        vi_ = ((h // 2) % 2) * 2 + (h % 2)
                        sy.op('pe', lambda e: e.matmul(ps_o[ob_][0:65, 0:n], lhsT=VP[vi_][:, kt, :],
                                                       rhs=PB[sb_][:, 0:n], start=(ci == 0), stop=(ci == nk - 1)),
                              r=[('VP', vi_), ('PB', sb_)], w=[('ps_o', ob_)], sig=True)
                        if ci == nk - 1:
                            epilogue(ob_, h, qblk)
                assert not pend
                sy.barrier()

        def phase_out(L, first, last, need_ctx):
            i = L // 2
            wsrc = ev_w_out if L % 2 == 0 else od_w_out
            with contextlib.ExitStack() as ps:
                wout = sbuf(ps, "wout", [128, 8, D], BF16)
                st = {'wi': 0}
                with contextlib.ExitStack() as wst:
                    stg = [sbuf(wst, "stgc%d" % b, [128, D], F32) for b in range(2)]
                    load_cast_weight(wst, wout, lambda k: wsrc[i, k * 128:(k + 1) * 128, :], D, 'wout', stg, st, 8)
                    sy.barrier()
                oTt = [sbuf(ps, "oTt%d" % b, [128, 4, 512], BF16) for b in range(2)]
                gTt = [sbuf(ps, "gTt%d" % b, [128, 4, 512], BF16) for b in range(2)]
                mix = [sbuf(ps, "mix%d" % b, [128, 4, 512], BF16) for b in range(2)]
                mixF = [sbuf(ps, "mixF%d" % b, [128, 8, 512], BF16) for b in range(2)]
                xt = [sbuf(ps, "xtc%d" % b, [128, D], F32) for b in range(2)]
                yt = [sbuf(ps, "ytc%d" % b, [128, D], F32) for b in range(2)]
                xn = [sbuf(ps, "xnc%d" % b, [128, D], F32) for b in range(2)]
                junk = sbuf(ps, "junkc", [128, D], BF16)
                ssq = [sbuf(ps, "ssqc%d" % b, [128, 1], F32) for b in range(2)]
                ot = [sbuf(ps, "otc%d" % b, [128, D], F32) for b in range(2)]
                ps_y = [psum(ps, "ps_y%d" % b, [128, 512]) for b in range(4)]
                cnt = {'y': 0, 'x': 0, 'm': 0, 'f': 0}
                oTv = oT_s.rearrange("(k p) t -> p k t", p=128)
                gTv = gT_s.rearrange("(k p) t -> p k t", p=128)

                def chunk_aps(ck):
                    if ck[0] < C:
                        return mixp_c, mixf_c
                    ci = (ck[0] - C) // 1024
                    return mixp[ci], mixf[ci]

                def stage_a(ck):
                    c0, cn = ck
                    mp, mf = chunk_aps(ck)
                    mpv = mp.rearrange("(k p) t -> p k t", p=128)
                    for r0 in range(c0, c0 + cn, 512):
                        n = min(512, c0 + cn - r0)
                        bb = cnt['m'] % 2
                        cnt['m'] += 1
                        sy.dma('sp', oTt[bb][:, :, 0:n], oTv[:, :, r0:r0 + n], w=[('oTt', bb)])
                        sy.dma('sp', gTt[bb][:, :, 0:n], gTv[:, :, r0:r0 + n], w=[('gTt', bb)])
                        sy.op('dve', lambda e: e.tensor_tensor(out=mix[bb][:, :, 0:n], in0=oTt[bb][:, :, 0:n],
                                                               in1=gTt[bb][:, :, 0:n], op=ALU.mult),
                              r=[('oTt', bb), ('gTt', bb)], w=[('mix', bb)])
                        sy.dma('pool', mpv[:, :, r0 - c0:r0 - c0 + n], mix[bb][:, :, 0:n], r=[('mix', bb)],
                               w=[('mp', c0)])
                    sy.allgather(mf, mp, r=[('mp', c0)], w=[('mf', c0)])

                def stage_b(ck):
                    c0, cn = ck
                    mp, mf = chunk_aps(ck)
                    mfv = mf.rearrange("(k p) t -> p k t", p=128)
                    for r0 in range(c0, c0 + cn, 512):
                        n = min(512, c0 + cn - r0)
                        fb = cnt['f'] % 2
                        cnt['f'] += 1
                        sy.dma('sp', mixF[fb][:, :, 0:n], mfv[:, :, r0 - c0:r0 - c0 + n], r=[('mf', c0)],
                               w=[('mixF', fb)])
                        for j in range(n // 128):
                            g = r0 // 128 + j
                            rr = "c" if g < 2 else "l"
                            xs = cnt['x'] % 2
                            cnt['x'] += 1
                            sy.dma('sp', xt[xs][:], x_rows(first, g * 128, 128), w=[('xt', xs)])
                            for hf in range(2):
                                b = cnt['y'] % 4
                                cnt['y'] += 1
                                for k in range(8):
                                    sy.op('pe', lambda e, k=k: e.matmul(
                                        ps_y[b][:, :], lhsT=mixF[fb][:, k, j * 128:(j + 1) * 128],
                                        rhs=wout[:, k, hf * 512:(hf + 1) * 512], start=(k == 0), stop=(k == 7)),
                                          r=[('mixF', fb), 'wout'], w=[('ps_y', b)], sig=(k == 7))
                                sy.op('dve', lambda e: e.tensor_tensor(out=yt[xs][:, hf * 512:(hf + 1) * 512],
                                                                       in0=ps_y[b][:, :],
                                                                       in1=modt[("G", rr)][:, hf * 512:(hf + 1) * 512],
                                                                       op=ALU.mult),
                                      r=[('ps_y', b), ('mod', 'G', rr)], w=[('yt', xs)])
                            sy.op('pool', lambda e: e.tensor_tensor(out=xn[xs][:], in0=yt[xs][:], in1=xt[xs][:],
                                                                    op=ALU.add),
                                  r=[('yt', xs), ('xt', xs)], w=[('xn', xs)])
                            if not (last and final):
                                dst = xdst if last else xcur
                                sy.dma('pool', dst[g * 128:(g + 1) * 128, :], xn[xs][:], r=[('xn', xs)])
                            else:
                                sy.op('act', lambda e: e.activation(out=junk[:], in_=xn[xs][:], func=AF.Square,
                                                                    accum_out=ssq[xs][:]),
                                      r=[('xn', xs)], w=['junk', ('ssq', xs)])
                                sy.op('act', lambda e: e.activation(out=ssq[xs][:], in_=ssq[xs][:], func=AF.Sqrt,
                                                                    scale=1.0 / D, bias=epsb[:, 0:1]),
                                      r=[('ssq', xs), 'epsb'], w=[('ssq', xs)])
                                sy.op('dve', lambda e: e.reciprocal(out=ssq[xs][:], in_=ssq[xs][:]),
                                      r=[('ssq', xs)], w=[('ssq', xs)])
                                sy.op('dve', lambda e: e.scalar_tensor_tensor(out=ot[xs][:], in0=xn[xs][:],
                                                                              scalar=ssq[xs][:, 0:1], in1=fbc[:],
                                                                              op0=ALU.mult, op1=ALU.mult),
                                      r=[('xn', xs), ('ssq', xs), 'fbc'], w=[('ot', xs)])
                                sy.dma('pool', xdst[(g - 2) * 128:(g - 1) * 128, :], ot[xs][:], r=[('ot', xs)])

                chunks = ([(0, C)] if need_ctx else []) + [(C + 1024 * c_, 1024) for c_ in range(S // 1024)]
                prev = None
                for ck in chunks:
                    stage_a(ck)
                    if prev is not None:
                        stage_b(prev)
                    prev = ck
                stage_b(prev)
                sy.barrier()

        sy.barrier()
        for li, L in enumerate(layers):
            first = (li == 0)
            last = (li == len(layers) - 1)
            need_ctx = L < DEPTH - 1
            phase_mod(L)
            if L % 2 == 0:
                phase_even(L, first, need_ctx)
            else:
                phase_odd(L, first, need_ctx)
            phase_out(L, first, last, need_ctx)
        sy.barrier()
        build.ninst = sy.ninst
    return nc


def _rope_tables():
    n_freq = 8
    inv = (10000.0 ** (-np.arange(n_freq, dtype=np.float32) / n_freq)).astype(np.float32)
    t = np.arange(S)
    ang_r = (t // GW).astype(np.float32)[:, None] * inv
    ang_c = (t % GW).astype(np.float32)[:, None] * inv
    cr, sr, cc, sc = np.cos(ang_r), np.sin(ang_r), np.cos(ang_c), np.sin(ang_c)
    cosT = np.concatenate([cr, cr, cc, cc], axis=1).T
    sinT = np.concatenate([-sr, sr, -sc, sc], axis=1).T
    return np.ascontiguousarray(cosT, np.float32), np.ascontiguousarray(sinT, np.float32)


_ROPE_PERM = np.concatenate([np.arange(8, 16), np.arange(0, 8), np.arange(24, 32), np.arange(16, 24)])


def _na_compact_tables(rpb):
    cols = np.arange(GW)
    cs = np.clip(cols - 8, 0, GW - 16)
    kc = np.arange(GW)[:, None]
    qc = np.arange(GW)[None, :]
    valid = (kc >= cs[None, :]) & (kc < cs[None, :] + 16)
    dc = np.clip(kc - qc + 15, 0, 30)
    out = np.full((rpb.shape[0], rpb.shape[1], GW, 15, GW), MASKV, np.float32)
    for e in range(15):
        dr_idx = (7 - e) + 7
        g = rpb[:, :, dr_idx, :][:, :, dc]
        out[:, :, :, e, :] = np.where(valid[None, None], g, np.float32(MASKV))
    return np.ascontiguousarray(out.reshape(rpb.shape[0], rpb.shape[1], GW, 15 * GW))


def _core_inputs(inp, p):
    f = lambda a: np.ascontiguousarray(a, np.float32)
    cosT, sinT = _rope_tables()
    w_uq = inp['mla_w_uq'].reshape(2, 256, 8, 96)
    w_uqp = w_uq.copy()
    w_uqp[..., 64:] = w_uq[..., 64:][..., _ROPE_PERM]
    sel = np.zeros((2, 2, 128), np.float32)
    sel[0, 0, :] = 1.0
    sel[1, 1, :] = 1.0
    w_in = inp['ev_w_in']
    ev_w_in = np.concatenate([w_in[:, :, 0:416], w_in[:, :, 416 + 256 * p:416 + 256 * p + 256],
                              w_in[:, :, 928 + 256 * p:928 + 256 * p + 256],
                              w_in[:, :, 1440 + 256 * p:1440 + 256 * p + 256]], axis=2)
    w_out = inp['ev_w_out']
    ev_w_out = np.concatenate([w_out[:, 0:256], w_out[:, 512:768], w_out[:, 256:512], w_out[:, 768:1024]], axis=1)
    ow = inp['od_w_in']
    od_w_in = np.concatenate([ow[:, :, sec * 1024 + 512 * p:sec * 1024 + 512 * p + 512] for sec in range(4)], axis=2)

    def per_part(a):
        return f(a.reshape(2, 2, 4, 128).transpose(0, 3, 1, 2)[:, :, :, 2 * p:2 * p + 2])

    return {
        'ada_w': f(inp['ada_w']), 'ada_b': f(inp['ada_b']), 'norm_g': f(inp['norm_g']),
        'ev_w_in': f(ev_w_in),
        'ev_w_krp': f(inp['ev_w_in'][:, :, 384:416][:, :, _ROPE_PERM]),
        'q_norm': f(inp['mla_q_norm'].reshape(2, 2, 128).transpose(0, 2, 1)),
        'kv_norm': f(inp['mla_kv_norm'].reshape(2, 128, 1)),
        'w_uq': f(w_uq[:, :, 4 * p:4 * p + 4].reshape(2, 256, 384)),
        'w_uqp': f(w_uqp[:, :, 4 * p:4 * p + 4].reshape(2, 256, 384)),
        'w_ukv': f(inp['mla_w_ukv'].reshape(2, 128, 8, 128)[:, :, 4 * p:4 * p + 4].reshape(2, 128, 512)),
        'conv_w': f(inp['lru_conv_w'].reshape(2, 4, 4, 128).transpose(0, 3, 2, 1)[:, :, 2 * p:2 * p + 2, :]),
        'conv_b': f(inp['lru_conv_b'].reshape(2, 4, 128).transpose(0, 2, 1)[:, :, 2 * p:2 * p + 2]),
        'lru_wa': f(inp['lru_wa'][:, :, 4 * p:4 * p + 4]), 'lru_wx': f(inp['lru_wx'][:, :, 4 * p:4 * p + 4]),
        'lru_ba': per_part(inp['lru_ba']), 'lru_bx': per_part(inp['lru_bx']),
        'lru_lam': per_part(inp['lru_lambda']),
        'ev_w_out': f(ev_w_out), 'od_w_in': f(od_w_in),
        'na_ct': f(_na_compact_tables(np.asarray(inp['na_rpb'], np.float32))[:, 8 * p:8 * p + 8]),
        'od_w_out': f(inp['od_w_out']),
        'fin_g': f(np.broadcast_to(inp['final_norm_g'].reshape(1, D), (128, D))),
        'ident': np.eye(128, dtype=np.float32), 'sel': sel, 'cosT': cosT, 'sinT': sinT,
    }


_NC_CACHE = {}


def _get_nc(layers, final):
    key = (tuple(layers), final)
    if key not in _NC_CACHE:
        _NC_CACHE[key] = build(list(layers), final)
    return _NC_CACHE[key]


FUSED = True


def kernel(**inp):
    inp = {k: np.asarray(v) for k, v in inp.items()}
    common = [_core_inputs(inp, p) for p in range(2)]
    xs = []
    cts = []
    for c_ in range(NCORES):
        b = c_ // 2
        xs.append(np.ascontiguousarray(np.concatenate([inp['ctx'][b], inp['x'][b]], axis=0), np.float32))
        cv = np.stack([inp['c'][b], inp['c_ctx']], axis=1)
        cts.append(np.ascontiguousarray(cv.reshape(8, 128, 2).transpose(1, 0, 2), np.float32))
    groups = [list(range(DEPTH))] if FUSED else [[l] for l in range(DEPTH)]
    for gi, layers in enumerate(groups):
        final = (gi == len(groups) - 1)
        nc = _get_nc(layers, final)
        in_maps = []
        for c_ in range(NCORES):
            m = dict(common[c_ % 2])
            m['xsrc'] = xs[c_]
            m['cT'] = cts[c_]
            in_maps.append(m)
        res = run_bass_kernel_spmd(nc, in_maps, core_ids=list(range(NCORES)))
        if final:
            out = np.stack([np.asarray(res.results[2 * b]['out'], np.float32) for b in range(B)], axis=0)
            return out
        xs = [np.ascontiguousarray(res.results[c_]['xdst'], np.float32) for c_ in range(NCORES)]
```
